# Optimizing a Trainium2 kernel written in Bass

```python
import jax, jax.numpy as jnp
from jax import lax
import numpy as np

D_MODEL = 2048
BATCH = 32
SEQ = 256
DEPTH = 2
DEC_BATCH = 4
DEC_SEQ = 1024
PAST_LEN = 256

GRID_W = 64
HEAD_DIM = 128
H_A = 4
H_B = 8
KV_B = 2
G_B = H_B // KV_B
H_C = 4
NA_ROWS = 8
NA_COLS = 16
NA_QCOLS = 16
NA_UCOLS = NA_QCOLS + NA_COLS
Q_BLOCK = 128
RET_CHUNK = 64
ROPE_THETA = 10000.0
FFN_DIM = 5504
CONV_WIDTH = 3
RMS_EPS = 1e-6
NEG_INF = -1e30
PROJ_SIZES = (H_A * HEAD_DIM, H_A * HEAD_DIM, H_A * HEAD_DIM,
              H_B * HEAD_DIM, KV_B * HEAD_DIM, KV_B * HEAD_DIM,
              H_C * HEAD_DIM, H_C * HEAD_DIM, H_C * HEAD_DIM, H_C * HEAD_DIM)
IN_DIM = 3 * H_A * HEAD_DIM + (H_B + 2 * KV_B) * HEAD_DIM + 4 * H_C * HEAD_DIM
MIX_DIM = (H_A + H_B + H_C) * HEAD_DIM

kernel_name = 'hybrid_natten_gqa_retention_dit_step'

f32 = jnp.float32


def rms_norm(x, g):
    xf = x.astype(f32)
    y = xf * lax.rsqrt(jnp.mean(xf * xf, axis=-1, keepdims=True) + RMS_EPS)
    return (y * g.astype(f32)).astype(x.dtype)


def adaln(cond, w, b):
    mod = jax.nn.silu(cond) @ w + b
    return jnp.split(mod[:, None, :], 6, axis=-1)


def modulate(h, shift, scale):
    return h * (1.0 + scale) + shift


def in_projection(h, w_in):
    p = h @ w_in
    parts = []
    off = 0
    for size in PROJ_SIZES:
        parts.append(p[..., off:off + size])
        off += size
    return parts


def head_inputs(h, w_in, qn_a, kn_a, qn_b, kn_b):
    B, T, _ = h.shape
    qa, ka, va, qb, kb, vb, qc, kc, vc, gc = in_projection(h, w_in)
    heads = lambda t, n: t.reshape(B, T, n, HEAD_DIM)
    qa = rms_norm(heads(qa, H_A), qn_a)
    ka = rms_norm(heads(ka, H_A), kn_a)
    va = heads(va, H_A)
    qb = rms_norm(heads(qb, H_B), qn_b)
    kb = rms_norm(heads(kb, KV_B), kn_b)
    vb = heads(vb, KV_B)
    return qa, ka, va, qb, kb, vb, heads(qc, H_C), heads(kc, H_C), heads(vc, H_C), gc


def axial_rope_tables(T):
    t = jnp.arange(T)
    nf = HEAD_DIM // 4
    inv_freq = jnp.power(ROPE_THETA, -jnp.arange(nf, dtype=f32) / nf)
    rows = (t // GRID_W).astype(f32)
    cols = (t % GRID_W).astype(f32)
    ang = jnp.stack([rows[:, None] * inv_freq, cols[:, None] * inv_freq], axis=1)
    return jnp.cos(ang), jnp.sin(ang)


def apply_axial_rope(x, cos, sin):
    B, T, H, d = x.shape
    xr = x.astype(f32).reshape(B, T, H, 2, 2, d // 4)
    x1, x2 = xr[..., 0, :], xr[..., 1, :]
    c = cos[None, :, None]
    s = sin[None, :, None]
    out = jnp.stack([x1 * c - x2 * s, x1 * s + x2 * c], axis=-2)
    return out.reshape(B, T, H, d).astype(x.dtype)


def block_attention(q, k, v):
    B, Tq, Hkv, G, d = q.shape
    nb = Tq // Q_BLOCK
    qb = q.reshape(B, nb, Q_BLOCK, Hkv, G, d).transpose(1, 0, 2, 3, 4, 5)
    scale = d ** -0.5

    def one_block(qblk):
        s = jnp.einsum('bqhgd,bkhd->bhgqk', qblk, k, preferred_element_type=f32) * scale
        p = jax.nn.softmax(s, axis=-1).astype(v.dtype)
        return jnp.einsum('bhgqk,bkhd->bqhgd', p, v)

    o = lax.map(one_block, qb)
    return o.transpose(1, 0, 2, 3, 4, 5).reshape(B, Tq, Hkv, G, d)


def neighbourhood_attention(q, k, v, k_ctx, v_ctx, rel_bias):
    B, T, H, d = q.shape
    rows = T // GRID_W
    kr = min(NA_ROWS, rows)
    nb = GRID_W // NA_QCOLS
    r = jnp.arange(rows)
    row_idx = jnp.clip(r - kr // 2, 0, rows - kr)[:, None] + jnp.arange(kr)[None, :]
    blk = jnp.arange(nb)
    col_idx = (jnp.clip(blk * NA_QCOLS - NA_COLS // 2, 0, GRID_W - NA_UCOLS)[:, None]
               + jnp.arange(NA_UCOLS)[None, :])
    q_col = blk[:, None] * NA_QCOLS + jnp.arange(NA_QCOLS)[None, :]
    c_start = jnp.clip(q_col - NA_COLS // 2, 0, GRID_W - NA_COLS)[..., None]
    in_win = (col_idx[:, None, :] >= c_start) & (col_idx[:, None, :] < c_start + NA_COLS)
    dr = row_idx - r[:, None] + (NA_ROWS - 1)
    dc = jnp.clip(col_idx[:, None, :] - q_col[..., None] + (NA_COLS - 1), 0, 2 * NA_COLS - 2)
    bias = rel_bias[:, dr[:, :, None, None, None], dc[None, None]].astype(f32)
    bias = jnp.where(in_win[None, None, None], bias, NEG_INF).transpose(0, 1, 3, 4, 2, 5)
    kg = k.reshape(B, rows, GRID_W, H, d)[:, row_idx][:, :, :, col_idx]
    vg = v.reshape(B, rows, GRID_W, H, d)[:, row_idx][:, :, :, col_idx]
    qg = q.reshape(B, rows, nb, NA_QCOLS, H, d)
    scale = d ** -0.5
    s_loc = jnp.einsum('brnqhd,brknuhd->bhrnqku', qg, kg, preferred_element_type=f32) * scale + bias[None]
    n_loc = kr * NA_UCOLS
    s_loc = s_loc.reshape(B, H, rows, nb, NA_QCOLS, n_loc)
    s_ctx = jnp.einsum('brnqhd,bshd->bhrnqs', qg, k_ctx, preferred_element_type=f32) * scale
    p = jax.nn.softmax(jnp.concatenate([s_loc, s_ctx], axis=-1), axis=-1).astype(v.dtype)
    p_loc = p[..., :n_loc].reshape(B, H, rows, nb, NA_QCOLS, kr, NA_UCOLS)
    p_ctx = p[..., n_loc:]
    o = (jnp.einsum('bhrnqku,brknuhd->brnqhd', p_loc, vg)
         + jnp.einsum('bhrnqs,bshd->brnqhd', p_ctx, v_ctx))
    return o.reshape(B, T, H, d)


def retention_scan(q, k, v, log_gamma, state0):
    B, H, T, d = q.shape
    n = T // RET_CHUNK
    qc = q.reshape(B, H, n, RET_CHUNK, d)
    kc = k.reshape(B, H, n, RET_CHUNK, d)
    vc = v.reshape(B, H, n, RET_CHUNK, d)
    pos = jnp.arange(RET_CHUNK, dtype=f32)
    diff = pos[:, None] - pos[None, :]
    intra = jnp.where(diff >= 0, jnp.exp(log_gamma[:, None, None] * jnp.maximum(diff, 0.0)), 0.0)
    scores = jnp.einsum('bhnid,bhnjd->bhnij', qc, kc) * intra[None, :, None]
    o_inner = jnp.einsum('bhnij,bhnje->bhnie', scores, vc)
    k_w = jnp.exp(log_gamma[:, None] * (RET_CHUNK - 1.0 - pos))
    kv = jnp.einsum('bhnjd,bhnje->nbhde', kc * k_w[None, :, None, :, None], vc)
    chunk_decay = jnp.exp(log_gamma * RET_CHUNK)[None, :, None, None]

    def step(R, kv_n):
        return R * chunk_decay + kv_n, R

    R_final, R_prev = lax.scan(step, state0.astype(f32), kv)
    q_w = jnp.exp(log_gamma[:, None] * (pos + 1.0))
    o_cross = jnp.einsum('bhnid,nbhde->bhnie', qc * q_w[None, :, None, :, None], R_prev)
    return (o_inner + o_cross).reshape(B, H, T, d), R_final


def retention_mixer(qc, kc, vc, gc, decay_logit, norm_g, state_f, state_b):
    B, T = qc.shape[:2]
    to_bhtd = lambda t: t.astype(f32).transpose(0, 2, 1, 3)
    q = to_bhtd(qc)
    k = to_bhtd(kc) * (HEAD_DIM ** -0.5)
    v = to_bhtd(vc)
    log_gamma = jax.nn.log_sigmoid(decay_logit.astype(f32))
    flip = lambda t: jnp.flip(t, axis=2)
    o_f, s_f = retention_scan(q, k, v, log_gamma[0], state_f)
    o_b, s_b = retention_scan(flip(q), flip(k), flip(v), log_gamma[1], state_b)
    o = (o_f + flip(o_b)).transpose(0, 2, 1, 3)
    o = rms_norm(o, norm_g.reshape(H_C, HEAD_DIM)).reshape(B, T, H_C * HEAD_DIM)
    return o.astype(gc.dtype) * jax.nn.silu(gc), s_f, s_b


def conv_ffn(h, w_up, conv_w, conv_b, w_down):
    u = h @ w_up
    u = lax.conv_general_dilated(u, conv_w[:, None, :], window_strides=(1,),
                                 padding=((CONV_WIDTH // 2, CONV_WIDTH // 2),),
                                 dimension_numbers=('NWC', 'WIO', 'NWC'),
                                 feature_group_count=u.shape[-1]) + conv_b
    gate, val = jnp.split(u, 2, axis=-1)
    return (jax.nn.silu(gate) * val) @ w_down


def setup_inputs(seed: int = 0) -> dict:
    key = jax.random.key(seed)
    ks = jax.random.split(key, 32)
    nrm = lambda i, shape, s: jax.random.normal(ks[i], shape, f32) * s
    base_logit = jnp.log(jnp.exp2(5.0 + jnp.arange(H_C, dtype=f32)) - 1.0)
    return {
        'x_prompt': nrm(0, (BATCH, SEQ, D_MODEL), 1.0),
        'x_sample': nrm(1, (DEC_BATCH, DEC_SEQ, D_MODEL), 1.0),
        'cache_ka': nrm(2, (DEC_BATCH, DEPTH, PAST_LEN, H_A, HEAD_DIM), 1.0),
        'cache_va': nrm(3, (DEC_BATCH, DEPTH, PAST_LEN, H_A, HEAD_DIM), 1.0),
        'cache_kb': nrm(4, (DEC_BATCH, DEPTH, PAST_LEN, KV_B, HEAD_DIM), 1.0),
        'cache_vb': nrm(5, (DEC_BATCH, DEPTH, PAST_LEN, KV_B, HEAD_DIM), 1.0),
        'state_ret': nrm(6, (DEC_BATCH, DEPTH, 2, H_C, HEAD_DIM, HEAD_DIM), 0.5),
        'c': nrm(7, (DEC_BATCH, D_MODEL), 1.0),
        'c_ctx': nrm(8, (D_MODEL,), 1.0),
        'ada_w': nrm(9, (DEPTH, D_MODEL, 6 * D_MODEL), 0.5 * D_MODEL ** -0.5),
        'ada_b': nrm(10, (DEPTH, 6 * D_MODEL), 0.02),
        'norm_mix_g': 1.0 + nrm(11, (DEPTH, D_MODEL), 0.02),
        'norm_ffn_g': 1.0 + nrm(12, (DEPTH, D_MODEL), 0.02),
        'w_in': nrm(13, (DEPTH, D_MODEL, IN_DIM), D_MODEL ** -0.5),
        'q_norm_a': 1.0 + nrm(14, (DEPTH, HEAD_DIM), 0.02),
        'k_norm_a': 1.0 + nrm(15, (DEPTH, HEAD_DIM), 0.02),
        'q_norm_b': 1.0 + nrm(16, (DEPTH, HEAD_DIM), 0.02),
        'k_norm_b': 1.0 + nrm(17, (DEPTH, HEAD_DIM), 0.02),
        'na_rel_bias': nrm(18, (DEPTH, H_A, 2 * NA_ROWS - 1, 2 * NA_COLS - 1), 0.02),
        'ret_decay_logit': base_logit[None, None, :] + nrm(19, (DEPTH, 2, H_C), 0.1),
        'ret_norm_g': 1.0 + nrm(20, (DEPTH, H_C * HEAD_DIM), 0.02),
        'w_out': nrm(21, (DEPTH, MIX_DIM, D_MODEL), MIX_DIM ** -0.5),
        'w_up': nrm(22, (DEPTH, D_MODEL, 2 * FFN_DIM), D_MODEL ** -0.5),
        'conv_w': nrm(23, (DEPTH, CONV_WIDTH, 2 * FFN_DIM), CONV_WIDTH ** -0.5),
        'conv_b': nrm(24, (DEPTH, 2 * FFN_DIM), 0.01),
        'w_down': nrm(25, (DEPTH, FFN_DIM, D_MODEL), FFN_DIM ** -0.5),
    }


def reference(x_prompt, x_sample, cache_ka, cache_va, cache_kb, cache_vb, state_ret, c, c_ctx,
              ada_w, ada_b, norm_mix_g, norm_ffn_g, w_in, q_norm_a, k_norm_a, q_norm_b, k_norm_b,
              na_rel_bias, ret_decay_logit, ret_norm_g, w_out, w_up, conv_w, conv_b, w_down):
    xp = x_prompt
    xs = x_sample
    Bp, S, _ = xp.shape
    Bs, T, _ = xs.shape
    rope_cos, rope_sin = axial_rope_tables(T)
    ka_list, va_list, kb_list, vb_list, st_list = [], [], [], [], []
    for l in range(DEPTH):
        m_ctx = adaln(c_ctx[None, :], ada_w[l], ada_b[l])
        m_lat = adaln(c, ada_w[l], ada_b[l])

        h = modulate(rms_norm(xp, norm_mix_g[l]), m_ctx[0], m_ctx[1])
        qa, ka, va, qb, kb, vb, qc, kc, vc, gc = head_inputs(h, w_in[l], q_norm_a[l], k_norm_a[l],
                                                             q_norm_b[l], k_norm_b[l])
        oa = block_attention(qa[:, :, :, None, :], ka, va).reshape(Bp, S, H_A * HEAD_DIM)
        ob = block_attention(qb.reshape(Bp, S, KV_B, G_B, HEAD_DIM), kb, vb).reshape(Bp, S, H_B * HEAD_DIM)
        zero_state = jnp.zeros((Bp, H_C, HEAD_DIM, HEAD_DIM), f32)
        oc, s_f, s_b = retention_mixer(qc, kc, vc, gc, ret_decay_logit[l], ret_norm_g[l], zero_state, zero_state)
        xp = xp + m_ctx[2] * (jnp.concatenate([oa, ob, oc], axis=-1) @ w_out[l])
        h = modulate(rms_norm(xp, norm_ffn_g[l]), m_ctx[3], m_ctx[4])
        xp = xp + m_ctx[5] * conv_ffn(h, w_up[l], conv_w[l], conv_b[l], w_down[l])
        ka_list.append(ka)
        va_list.append(va)
        kb_list.append(kb)
        vb_list.append(vb)
        st_list.append(jnp.stack([s_f, s_b], axis=1))

        h = modulate(rms_norm(xs, norm_mix_g[l]), m_lat[0], m_lat[1])
        qa, ka, va, qb, kb, vb, qc, kc, vc, gc = head_inputs(h, w_in[l], q_norm_a[l], k_norm_a[l],
                                                             q_norm_b[l], k_norm_b[l])
        oa = neighbourhood_attention(qa, ka, va, cache_ka[:, l], cache_va[:, l],
                                     na_rel_bias[l]).reshape(Bs, T, H_A * HEAD_DIM)
        qb_r = apply_axial_rope(qb, rope_cos, rope_sin)
        kb_r = apply_axial_rope(kb, rope_cos, rope_sin)
        k_all = jnp.concatenate([kb_r, cache_kb[:, l]], axis=1)
        v_all = jnp.concatenate([vb, cache_vb[:, l]], axis=1)
        ob = block_attention(qb_r.reshape(Bs, T, KV_B, G_B, HEAD_DIM), k_all, v_all).reshape(Bs, T, H_B * HEAD_DIM)
        oc, _, _ = retention_mixer(qc, kc, vc, gc, ret_decay_logit[l], ret_norm_g[l],
                                   state_ret[:, l, 0], state_ret[:, l, 1])
        xs = xs + m_lat[2] * (jnp.concatenate([oa, ob, oc], axis=-1) @ w_out[l])
        h = modulate(rms_norm(xs, norm_ffn_g[l]), m_lat[3], m_lat[4])
        xs = xs + m_lat[5] * conv_ffn(h, w_up[l], conv_w[l], conv_b[l], w_down[l])

    new_cache_ka = jnp.stack(ka_list, axis=1)
    new_cache_va = jnp.stack(va_list, axis=1)
    new_cache_kb = jnp.stack(kb_list, axis=1)
    new_cache_vb = jnp.stack(vb_list, axis=1)
    new_state_ret = jnp.stack(st_list, axis=1)
    return (xp, xs, new_cache_ka, new_cache_va, new_cache_kb, new_cache_vb, new_state_ret)
```

```python
import numpy as np
from contextlib import ExitStack
import concourse.bass as bass
import concourse.mybir as mybir
from concourse.bass_utils import run_bass_kernel_spmd

F32 = mybir.dt.float32
BF16 = mybir.dt.bfloat16
AF = mybir.ActivationFunctionType
ALU = mybir.AluOpType

D = 2048
T = 1024
NK = 16
IN_DIM = 5120
FFN = 5504
NH = 43
EPS = 1e-6
SCALE = 128.0 ** -0.5
SEM_EPOCH = 6000

QA0, KA0, VA0, QB0, KB0, VB0, QC0, KC0, VC0, GC0 = 0, 512, 1024, 1536, 2560, 2816, 3072, 3584, 4096, 4608


class View:
    __slots__ = ("ap", "name", "ranges")

    def __init__(self, ap, name, ranges):
        self.ap = ap
        self.name = name
        self.ranges = ranges


class Tn:
    def __init__(self, name, base_ap, fshape, esize, base_byte):
        self.name = name
        self.base_ap = base_ap
        self.fshape = tuple(fshape)
        self.esize = esize
        self.base_byte = base_byte
        st = []
        s = 1
        for n in reversed(self.fshape):
            st.append(s)
            s *= n
        self.fstrides = tuple(reversed(st))

    def __getitem__(self, idx):
        if not isinstance(idx, tuple):
            idx = (idx,)
        ap = self.base_ap[idx]
        fidx = list(idx[1:]) + [slice(None)] * (len(self.fshape) - (len(idx) - 1))
        sel = []
        for i, n in zip(fidx, self.fshape):
            if isinstance(i, slice):
                a, b, c = i.indices(n)
                sel.append(list(range(a, b, c)))
            else:
                sel.append([i])
        last = sel[-1]
        lo_l, hi_l = last[0], last[-1] + 1
        bases = [0]
        for d in range(len(sel) - 1):
            bases = [b + j * self.fstrides[d] for b in bases for j in sel[d]]
        rs = []
        for b in bases:
            lo = self.base_byte + (b + lo_l) * self.esize
            hi = self.base_byte + (b + hi_l) * self.esize
            if rs and rs[-1][1] == lo:
                rs[-1] = (rs[-1][0], hi)
            else:
                rs.append((lo, hi))
        return View(ap, self.name, rs)


class Prog:
    ENG = ("pe", "act", "dve", "pool", "sp")

    def __init__(self, nc, stack):
        self.nc = nc
        self.stack = stack
        self.entries = {e: [] for e in self.ENG}
        self.count = {e: 0 for e in self.ENG}
        self.epoch = {e: 0 for e in self.ENG}
        self.sems = {}
        self.acc = {}
        self.seen = {e: {} for e in self.ENG}
        self.dma_cnt = {}
        self.out_tokens = {}
        self.n_ops = 0

    def sem(self, key):
        s = self.sems.get(key)
        if s is None:
            s = self.stack.enter_context(self.nc.semaphore("s%d" % len(self.sems)))
            self.sems[key] = s
        return s

    def _deps(self, reads, writes):
        waits = {}
        for v in reads:
            lst = self.acc.get(v.name)
            if not lst:
                continue
            for (lo, hi) in v.ranges:
                for e in lst:
                    if e[3] and e[0] < hi and lo < e[1]:
                        k, c = e[2]
                        if waits.get(k, 0) < c:
                            waits[k] = c
        for v in writes:
            lst = self.acc.get(v.name)
            if not lst:
                continue
            for (lo, hi) in v.ranges:
                for e in lst:
                    if e[0] < hi and lo < e[1]:
                        k, c = e[2]
                        if waits.get(k, 0) < c:
                            waits[k] = c
        return waits

    def _record(self, reads, writes, tok):
        for v in writes:
            lst = self.acc.setdefault(v.name, [])
            for (lo, hi) in v.ranges:
                lst[:] = [e for e in lst if not (lo <= e[0] and e[1] <= hi)]
                lst.append((lo, hi, tok, True))
        for v in reads:
            lst = self.acc.setdefault(v.name, [])
            for (lo, hi) in v.ranges:
                k = tok[0]
                if k[0] == "E":
                    lst[:] = [e for e in lst if not (e[0] == lo and e[1] == hi and (not e[3]) and e[2][0] == k)]
                lst.append((lo, hi, tok, False))

    def _filter(self, eng, waits):
        out = []
        seen = self.seen[eng]
        for k, c in waits.items():
            if k[0] == "E":
                if eng == "pe" and k[1] == "pe":
                    continue
                mx = seen.get(("EP", k[1]), -1)
                if k[2] < mx:
                    continue
            if seen.get(k, 0) >= c:
                continue
            seen[k] = c
            if k[0] == "E":
                seen[("EP", k[1])] = max(seen.get(("EP", k[1]), -1), k[2])
            out.append((self.sem(k), c))
        return out

    @staticmethod
    def _norm(reads, writes):
        r2 = []
        w2 = []
        for v in reads:
            if v.name.startswith("ps"):
                w2.append(View(v.ap, v.name, [(0, 2048)]))
            else:
                r2.append(v)
        for v in writes:
            if v.name.startswith("ps"):
                w2.append(View(v.ap, v.name, [(0, 2048)]))
            else:
                w2.append(v)
        return r2, w2

    def op(self, eng, fn, reads=(), writes=()):
        reads, writes = self._norm(reads, writes)
        waits = self._filter(eng, self._deps(reads, writes))
        if self.count[eng] >= SEM_EPOCH:
            self.epoch[eng] += 1
            self.count[eng] = 0
        self.count[eng] += 1
        key = ("E", eng, self.epoch[eng])
        tok = (key, self.count[eng])
        self._record(reads, writes, tok)
        self.entries[eng].append((waits, fn, self.sem(key), 1))
        self.n_ops += 1
        return tok

    def dma(self, q, out, in_, key, is_output=False):
        reads = [in_] if isinstance(in_, View) else []
        writes = [out] if isinstance(out, View) else []
        waits = self._filter(q, self._deps(reads, writes))
        k = ("D", key)
        self.dma_cnt[k] = self.dma_cnt.get(k, 0) + 16
        tok = (k, self.dma_cnt[k])
        self._record(reads, writes, tok)
        oap = out.ap if isinstance(out, View) else out
        iap = in_.ap if isinstance(in_, View) else in_
        self.entries[q].append((waits, lambda e: e.dma_start(out=oap, in_=iap), self.sem(k), 16))
        if is_output:
            self.out_tokens[k] = self.dma_cnt[k]
        self.n_ops += 1
        return tok

    def finish(self):
        waits = [(self.sem(k), c) for k, c in self.out_tokens.items()]
        self.entries["sp"].append((waits, None, None, 0))

    def replay(self, eng, e):
        for waits, fn, sem, inc in self.entries[eng]:
            for s, c in waits:
                e.wait_ge(s, c)
            if fn is None:
                continue
            ins = fn(e)
            ins.then_inc(sem, inc)


def build_program(cfg=None):
    cfg = cfg or {}
    groups = cfg.get("groups", ("P", "S"))
    layers = cfg.get("layers", (0, 1))
    do_ffn = cfg.get("ffn", True)
    do_mix = cfg.get("mix", True)

    nc = bass.Bass("TRN2", target_bir_lowering=False)
    dram = {}

    def din(name, shape):
        dram[name] = nc.dram_tensor(name, list(shape), F32, kind="ExternalInput")
        return dram[name].ap()

    def dout(name, shape):
        dram[name] = nc.dram_tensor(name, list(shape), F32, kind="ExternalOutput")
        return dram[name].ap()

    xp_d = din("xp", [T, D])
    xs_d = din("xs", [T, D])
    condT_d = din("condT", [128, NK * 2])
    cka_d = din("cka", [2, 256, 512])
    cva_d = din("cva", [2, 256, 512])
    ckb_d = din("ckb", [2, 256, 256])
    cvb_d = din("cvb", [2, 256, 256])
    sret_d = din("sret", [2, 2, 4, 128, 128])
    adaw_d = din("ada_w", [2, D, 6 * D])
    win_d = din("w_in", [2, D, IN_DIM])
    wout_d = din("w_out", [2, D, D])
    wup_d = din("w_up", [2, D, 2 * FFN])
    wdn_d = din("w_down", [2, FFN, D])
    adabT_d = din("adabT", [128, 2 * 96])
    gmixT_d = din("gmixT", [128, 2 * 16])
    gffnT_d = din("gffnT", [128, 2 * 16])
    qkn_d = din("qkn", [128, 8])
    retgT_d = din("retgT", [128, 8])
    convwT_d = din("convwT", [128, 2 * 3 * 86])
    convbT_d = din("convbT", [128, 2 * 86])
    dlog_d = din("dlog", [128, 16])
    mbf_d = din("mbf", [2, 4, 15, 64, 64])
    identf_d = din("identf", [128, 128])
    perm_d = din("perm", [128, 128])
    ropeC_d = din("ropeC", [128, T])
    ropeS_d = din("ropeS", [128, T])
    wmask_d = din("wmask", [128, 64])
    xpm_d = din("xpm", [128, 1920])
    xnm_d = din("xnm", [128, 1920])
    t1f_d = din("t1f", [128, T])
    t1b_d = din("t1b", [128, T])
    pos_d = din("pos", [128, 4])

    yp_d = dout("yp", [T, D])
    ys_d = dout("ys", [T, D])
    nka_d = dout("nka", [4, 2, 256, 512])
    nva_d = dout("nva", [4, 2, 256, 512])
    nkb_d = dout("nkb", [4, 2, 256, 256])
    nvb_d = dout("nvb", [4, 2, 256, 256])
    nst_d = dout("nst", [4, 2, 2, 4, 128, 128])

    stack = ExitStack()
    ARENA_BYTES = 211968
    arena = stack.enter_context(nc.sbuf_tensor("arena", [128, ARENA_BYTES // 2], BF16))
    psb = [stack.enter_context(nc.psum_tensor("psb%d" % i, [128, 512], F32)) for i in range(8)]
    P = Prog(nc, stack)

    carve_off = [0]

    def carve_at(off, fshape, dt):
        es = 4 if dt == F32 else 2
        n = 1
        for s in fshape:
            n *= s
        nb = n * es
        assert off % 4 == 0 and off + nb <= ARENA_BYTES, (off, nb)
        ap = arena[:, off // 2: (off + nb) // 2]
        if dt == F32:
            ap = ap.bitcast(F32)
        if len(fshape) == 2:
            ap = ap.rearrange("p (a b) -> p a b", a=fshape[0])
        elif len(fshape) == 3:
            ap = ap.rearrange("p (a b c) -> p a b c", a=fshape[0], b=fshape[1])
        elif len(fshape) == 4:
            ap = ap.rearrange("p (a b c d) -> p a b c d", a=fshape[0], b=fshape[1], c=fshape[2])
        return Tn("arena", ap, fshape, es, off)

    def carve(nbytes):
        off = carve_off[0]
        nbytes = (nbytes + 63) // 64 * 64
        carve_off[0] += nbytes
        assert carve_off[0] <= ARENA_BYTES, carve_off[0]
        return off

    def pst(b, fshape=(512,)):
        ap = psb[b][:, 0:int(np.prod(fshape))]
        if len(fshape) == 2:
            ap = ap.rearrange("p (a b) -> p a b", a=fshape[0])
        elif len(fshape) == 3:
            ap = ap.rearrange("p (a b c) -> p a b c", a=fshape[0], b=fshape[1])
        return Tn("ps%d" % b, ap, fshape, 4, 0)

    XT_OFF = carve(65536)
    HT_OFF = carve(32768)
    MIX_OFF = carve(32768)
    SLAB_OFF = [carve(16384), carve(16384), None]
    QKV_OFF = carve(20480)
    SLAB_OFF[2] = QKV_OFF
    S1_OFF = carve(8192)
    PT_OFF = carve(2048)
    MBG_OFF = carve(4096)
    OST_OFF = carve(4096)
    CONST_OFF = carve(1280)
    MODV_OFF = carve(1536 + 1536 + 768)
    PAR_OFF = carve(4096)

    xT = carve_at(XT_OFF, (NK, T), F32)
    hT = carve_at(HT_OFF, (NK, T), BF16)
    mixT = carve_at(MIX_OFF, (NK, T), BF16)
    identb = carve_at(CONST_OFF, (128,), BF16)
    onesb = carve_at(CONST_OFF + 256, (128,), BF16)
    identf = carve_at(CONST_OFF + 512, (128,), F32)
    permb = carve_at(CONST_OFF + 1024, (128,), BF16)
    modT = carve_at(MODV_OFF, (2, 96, 2), F32)
    modA = carve_at(MODV_OFF + 1536, (2, 2, 6, 16), F32)
    adabT = carve_at(MODV_OFF + 3072, (2, 96), F32)
    po = PAR_OFF
    convwT = carve_at(po, (2, 3, 86), F32); po += 2064
    convbT = carve_at(po, (2, 86), F32); po += 688
    gmixT = carve_at(po, (2, 16), F32); po += 128
    gffnT = carve_at(po, (2, 16), F32); po += 128
    qkn = carve_at(po, (2, 4), F32); po += 32
    retgT = carve_at(po, (2, 4), F32); po += 32
    dlog = carve_at(po, (16,), F32); po += 64
    lgam = carve_at(po, (16,), F32); po += 64
    wcol = carve_at(po, (4,), F32); po += 16
    posT = carve_at(po, (4,), F32); po += 16
    scT = carve_at(po, (NK, 2), BF16); po += 64
    condT = carve_at(po, (NK, 2), F32); po += 128
    wmask = carve_at(po, (64,), F32); po += 256
    assert po <= PAR_OFF + 4096, po

    def slab_dma(dst, src, key, nsplit=2):
        nk_ = dst.ap.shape[1]
        assert src.shape[1] == nk_, (dst.ap.shape, src.shape)
        bounds = [round(i_ * nk_ / nsplit) for i_ in range(nsplit + 1)]
        for a_, b_ in zip(bounds[:-1], bounds[1:]):
            if b_ > a_:
                P.dma("pool", View(dst.ap[:, a_:b_], dst.name, dst.ranges), src[:, a_:b_], key)

    def slab(slot, fshape, dt=BF16):
        return carve_at(SLAB_OFF[slot], fshape, dt)

    def mm(out, pairs, start=True, stop=True):
        n = len(pairs)

        def fn(e):
            ins = None
            for i, (l, r) in enumerate(pairs):
                ins = e.matmul(out.ap, lhsT=l.ap, rhs=r.ap, start=(start and i == 0), stop=(stop and i == n - 1))
            return ins
        rd = []
        for l, r in pairs:
            rd.append(l)
            rd.append(r)
            assert tuple(out.ap.shape[1:]) == tuple(r.ap.shape[1:]) or int(np.prod(out.ap.shape[1:])) == int(np.prod(r.ap.shape[1:])), (out.ap.shape, l.ap.shape, r.ap.shape)
            assert out.ap.shape[0] == l.ap.shape[-1] or len(l.ap.shape) > 2, (out.ap.shape, l.ap.shape, r.ap.shape)
        return P.op("pe", fn, reads=rd, writes=[out])

    def mm_multi(groups):
        def fn(e):
            ins = None
            for out, pairs in groups:
                n = len(pairs)
                for i, (l_, r_) in enumerate(pairs):
                    ins = e.matmul(out.ap, lhsT=l_.ap, rhs=r_.ap, start=(i == 0), stop=(i == n - 1))
            return ins
        rd = []
        wr = []
        for out, pairs in groups:
            wr.append(out)
            for l_, r_ in pairs:
                rd.append(l_)
                rd.append(r_)
        return P.op("pe", fn, reads=rd, writes=wr)

    def act(out, in_, func, scale=1.0, bias=0.0, eng="act"):
        rd = [in_]
        sc = scale
        bi = bias
        if isinstance(scale, View):
            rd.append(scale)
            sc = scale.ap
        if isinstance(bias, View):
            rd.append(bias)
            bi = bias.ap
        return P.op("act", lambda e: e.activation(out=out.ap, in_=in_.ap, func=func, scale=sc, bias=bi),
                    reads=rd, writes=[out])

    def tcopy(eng, out, in_):
        return P.op(eng, lambda e: e.tensor_copy(out=out.ap, in_=in_.ap), reads=[in_], writes=[out])

    def tt(eng, out, in0, in1, op):
        return P.op(eng, lambda e: e.tensor_tensor(out=out.ap, in0=in0.ap, in1=in1.ap, op=op),
                    reads=[in0, in1], writes=[out])

    def ts(eng, out, in0, s1, s2, op0, op1=None):
        rd = [in0]
        a1 = s1
        a2 = s2
        if isinstance(s1, View):
            rd.append(s1)
            a1 = s1.ap
        if isinstance(s2, View):
            rd.append(s2)
            a2 = s2.ap
        if op1 is None:
            return P.op(eng, lambda e: e.tensor_scalar(out=out.ap, in0=in0.ap, scalar1=a1, scalar2=None, op0=op0),
                        reads=rd, writes=[out])
        return P.op(eng, lambda e: e.tensor_scalar(out=out.ap, in0=in0.ap, scalar1=a1, scalar2=a2, op0=op0, op1=op1),
                    reads=rd, writes=[out])

    def stt(eng, out, in0, sc, in1, op0, op1):
        rd = [in0, in1]
        a = sc
        if isinstance(sc, View):
            rd.append(sc)
            a = sc.ap
        return P.op(eng, lambda e: e.scalar_tensor_tensor(out=out.ap, in0=in0.ap, scalar=a, in1=in1.ap, op0=op0, op1=op1),
                    reads=rd, writes=[out])

    def recip(out, in_):
        return P.op("dve", lambda e: e.reciprocal(out=out.ap, in_=in_.ap), reads=[in_], writes=[out])

    nsmall = [0]

    def load_small(tn, dap, key=None):
        nsmall[0] += 1
        P.dma("sp", tn[:], dap, key or ("small%d" % nsmall[0]))

    alt = [0]

    def evac_eng():
        alt[0] ^= 1
        return "dve" if alt[0] else "act"

    def copy_any(out, in_):
        if evac_eng() == "dve":
            tcopy("dve", out, in_)
        else:
            act(out, in_, AF.Copy)

    tmpc = carve_at(S1_OFF, (128,), F32)
    P.dma("sp", tmpc[:], identf_d, "smallA")
    tcopy("dve", identb[:], tmpc[:])
    tcopy("dve", identf[:], tmpc[:])
    tmpc2 = carve_at(S1_OFF + 512, (128,), F32)
    P.dma("sp", tmpc2[:], perm_d, "smallB")
    tcopy("dve", permb[:], tmpc2[:])
    P.op("dve", lambda e: e.memset(onesb[:].ap, 1.0), writes=[onesb[:]])
    load_small(adabT, adabT_d.rearrange("p (a b) -> p a b", a=2))
    load_small(convwT, convwT_d.rearrange("p (a b c) -> p a b c", a=2, b=3))
    load_small(convbT, convbT_d.rearrange("p (a b) -> p a b", a=2))
    load_small(gmixT, gmixT_d.rearrange("p (a b) -> p a b", a=2))
    load_small(gffnT, gffnT_d.rearrange("p (a b) -> p a b", a=2))
    load_small(qkn, qkn_d.rearrange("p (a b) -> p a b", a=2))
    load_small(retgT, retgT_d.rearrange("p (a b) -> p a b", a=2))
    load_small(dlog, dlog_d)
    load_small(posT, pos_d)
    load_small(condT, condT_d.rearrange("p (a b) -> p a b", a=NK))
    load_small(wmask, wmask_d)
    act(lgam[:], dlog[:], AF.Exp, scale=-1.0)
    onev = carve_at(PAR_OFF + 4016, (1,), F32)
    P.op("dve", lambda e: e.memset(onev[:].ap, 1.0), writes=[onev[:]])
    act(lgam[:], lgam[:], AF.Ln, bias=onev[:])
    ts("dve", lgam[:], lgam[:], -1.0, None, ALU.mult)
    act(scT[:], condT[:], AF.Silu)

    steps = []

    def add_step(load, compute, ring=(0, 1)):
        steps.append((load, compute, ring))
        if bg_on[0]:
            bg_emit_one(ring)

    def ada_slab(l, s):
        def load(slot):
            sl = slab(slot, (NK, 512))
            src = adaw_d[l].rearrange("(k p) c -> p k c", p=128)[:, :, s * 512:(s + 1) * 512]
            slab_dma(sl[:], src, "slab%d" % slot)

        def comp(slot):
            sl = slab(slot, (NK, 512))
            ps = pst(s % 3, (4, 128))
            for j in range(4):
                mm(ps[:, j, 0:2], [(sl[:, k, j * 128:(j + 1) * 128], scT[:, k, :]) for k in range(NK)])
            for j in range(4):
                ts("dve", modT[:, l, 4 * s + j, :], ps[:, j, 0:2], adabT[:, l, 4 * s + j: 4 * s + j + 1], None, ALU.add)
        return load, comp

    def ada_fin(l, kinds):
        for cnd in range(2):
            for (kind, src_blk, g) in ((0, 1, gmixT), (3, 4, gffnT)):
                if kind in kinds:
                    stt("dve", modA[:, l, cnd, kind, :], modT[:, l, src_blk * 16:(src_blk + 1) * 16, cnd], 1.0,
                        g[:, l, :], ALU.add, ALU.mult)
            for (kind, src_blk) in ((1, 0), (2, 2), (4, 3), (5, 5)):
                if kind in kinds:
                    tcopy("dve", modA[:, l, cnd, kind, :], modT[:, l, src_blk * 16:(src_blk + 1) * 16, cnd])

    bg = []
    for l_ in layers:
        for (lo_, hi_, kinds_) in ((0, 8, (0, 1)), (8, 12, (2,)), (12, 20, (3, 4)), (20, 24, (5,))):
            for s_ in range(lo_, hi_):
                bg.append(("slab", l_, s_))
            bg.append(("fin", l_, kinds_))
    bg_pos = [0]
    bg_on = [False]

    def bg_emit_one(ring):
        if bg_pos[0] >= len(bg):
            return
        item = bg[bg_pos[0]]
        bg_pos[0] += 1
        if item[0] == "slab":
            ld, cp = ada_slab(item[1], item[2])
            steps.append((ld, cp, ring))
        else:
            steps.append((None, (lambda slot, it=item: ada_fin(it[1], it[2])), ring))

    def bg_require(l, kind, ring=(0, 1)):
        tgt = None
        for i_, it in enumerate(bg):
            if it[0] == "fin" and it[1] == l and kind in it[2]:
                tgt = i_
        while bg_pos[0] <= tgt:
            bg_emit_one(ring)

    def load_x_steps(x_d):
        def comp(slot):
            for tti in range(8):
                stg = carve_at(MIX_OFF + (tti % 2) * 8192, (D,), F32)
                P.dma("sp", stg[:], x_d[tti * 128:(tti + 1) * 128, :], "xstg%d" % (tti % 2))
                for g in range(4):
                    ps = pst((tti * 4 + g) % 3, (4, 128))
                    for j in range(4):
                        k = 4 * g + j
                        mm(ps[:, j, :], [(stg[:, k * 128:(k + 1) * 128], identf[:])])
                    copy_any(xT[:, 4 * g:4 * g + 4, tti * 128:(tti + 1) * 128], ps[:])
        add_step(None, comp)

    def store_y_steps(y_d):
        def comp(slot):
            n = 0
            for tti in range(8):
                for g in range(4):
                    ps = pst(n % 3, (4, 128))
                    for j in range(4):
                        k = 4 * g + j
                        mm(ps[:, j, :], [(xT[:, k, tti * 128:(tti + 1) * 128], identf[:])])
                    stg = carve_at(MIX_OFF + (n % 4) * 2048, (512,), F32)
                    copy_any(stg[:], pst(n % 3)[:])
                    P.dma("sp", y_d[tti * 128:(tti + 1) * 128, g * 512:(g + 1) * 512], stg[:], "ystg%d" % (n % 4), is_output=True)
                    n += 1
        add_step(None, comp)

    rstd = carve_at(S1_OFF, (T,), F32)
    ntmp = carve_at(S1_OFF + 4096, (2, 512), F32)
    sqb = carve_at(PT_OFF, (2, 512), BF16)

    def norm_mod_steps(l, cnd, ka, kb_):
        def comp(slot):
            for h in range(2):
                hs = slice(h * 512, (h + 1) * 512)
                psn = pst(3)
                for k in range(NK):
                    act(sqb[:, k % 2, :], xT[:, k, hs], AF.Square)
                    mm(psn[:], [(onesb[:], sqb[:, k % 2, :])], start=(k == 0), stop=(k == NK - 1))
                act(rstd[:, hs], psn[:], AF.Ln, scale=1.0 / D, bias=epsv[:])
                act(rstd[:, hs], rstd[:, hs], AF.Exp, scale=-0.5)
                for k in range(NK):
                    stt("dve", ntmp[:, k % 2, :], xT[:, k, hs], modA[:, l, cnd, ka, k:k + 1], rstd[:, hs], ALU.mult, ALU.mult)
                    act(hT[:, k, hs], ntmp[:, k % 2, :], AF.Identity, bias=modA[:, l, cnd, kb_, k:k + 1])
        add_step(None, comp)

    def win_src(l, pieces, w):
        base = win_d[l].rearrange("(k p) c -> p k c", p=128)
        if len(pieces) == 1:
            return base[:, :, pieces[0]:pieces[0] + w]
        step = pieces[1] - pieces[0]
        for i in range(len(pieces) - 1):
            assert pieces[i + 1] - pieces[i] == step
        b0 = base[:, :, pieces[0]:pieces[0] + w]
        a = b0.ap
        return bass.AP(b0.tensor, b0.offset, [list(a[0]), list(a[1]), [step, len(pieces)], list(a[2])])

    tmpS = carve_at(S1_OFF, (2, 512), F32)
    rcp = carve_at(S1_OFF + 4096, (512,), F32)
    sq2 = carve_at(S1_OFF + 6144, (2, 512), BF16)
    pt = carve_at(PT_OFF, (2, 512), BF16)
    ostg = carve_at(OST_OFF, (2, 512), F32)
    nrm_cnt = [0]

    pending = []
    PROJ_BANKS = (0, 1, 2, 4, 5)

    def pbank():
        b = PROJ_BANKS[nrm_cnt[0] % len(PROJ_BANKS)]
        nrm_cnt[0] += 1
        return b

    def pipe_mm(ps_view, pairs, post):
        mm(ps_view, pairs)
        stages = list(post) if isinstance(post, (list, tuple)) else [post]
        for ent in list(pending):
            ent.pop(0)()
            if not ent:
                pending.remove(ent)
        pending.append(stages)

    def pipe_flush():
        while pending:
            for ent in list(pending):
                ent.pop(0)()
                if not ent:
                    pending.remove(ent)

    def norm_stages(ps, i2, gvec, dst, f32_dst=None, after=None):
        psn = pst(3 if i2 == 0 else 6)

        def stA():
            act(sq2[:, i2, :], ps[:], AF.Square)
            mm(psn[:], [(onesb[:], sq2[:, i2, :])])

        def stB():
            act(tmpS[:, i2, :], psn[:], AF.Ln, scale=1.0 / 128, bias=epsv[:])
            act(tmpS[:, i2, :], tmpS[:, i2, :], AF.Exp, scale=-0.5)
            if f32_dst is not None:
                stt("dve", f32_dst, ps[:], gvec, tmpS[:, i2, :], ALU.mult, ALU.mult)
                copy_any(dst, f32_dst)
            else:
                stt("dve", dst, ps[:], gvec, tmpS[:, i2, :], ALU.mult, ALU.mult)
            if after is not None:
                after()
        return [stA, stB]

    def proj_fm(dst_fn, sl, piece, ncol0, gvec, after=None):
        for h in range(2):
            hs = slice(h * 512, (h + 1) * 512)
            ps = pst(pbank())
            i2 = nrm_cnt[0] % 2
            aft = (lambda h=h: after(h)) if after is not None else None
            if gvec is None:
                def post(ps=ps, h=h, aft=aft):
                    copy_any(dst_fn(h), ps[:])
                    if aft is not None:
                        aft()
            else:
                post = norm_stages(ps, i2, gvec, dst_fn(h), after=aft)
            pipe_mm(ps[:], [(sl[:, k, piece, ncol0:ncol0 + 128], hT[:, k, hs]) for k in range(NK)], post)

    def proj_tm(sl_fn, ncols, dst_fn, tiles=range(8), tok_off=0, out_fn=None):
        for tti in tiles:
            ps = pst(pbank())
            t0 = tok_off + tti * 128

            def post(ps=ps, tti=tti):
                copy_any(dst_fn(tti), ps[:, 0:ncols])
                if out_fn is not None:
                    out_fn(tti, ps)
            pipe_mm(ps[:, 0:ncols], [(hT[:, k, t0:t0 + 128], sl_fn(k)) for k in range(NK)], post)

    ablk = [0]

    def attn_softmax(qv, nq, keys, dst, bias_fn=None):
        nkt = len(keys)
        sel = ablk[0] % 2
        ablk[0] += 1
        psO = pst(6 if sel == 0 else 0)
        psD = pst(7 if sel == 0 else 1)

        def emit_S(i):
            kv, vv = keys[i]
            psS = pst(4 + (i % 2))
            mm(psS[:, 0:nq], [(kv, qv)])
            act(pt[:, i % 2, 0:nq], psS[:, 0:nq], AF.Exp, scale=SCALE)

        def emit_PV(i):
            kv, vv = keys[i]
            mm(psO[:, 0:nq], [(vv, pt[:, i % 2, 0:nq])], start=(i == 0), stop=(i == nkt - 1))
            mm(psD[:, 0:nq], [(onesb[:], pt[:, i % 2, 0:nq])], start=(i == 0), stop=(i == nkt - 1))
        emit_S(0)
        for i in range(nkt):
            if i + 1 < nkt:
                emit_S(i + 1)
            emit_PV(i)
        recip(rcp[:, 0:nq], psD[:, 0:nq])
        tt("dve", dst, psO[:, 0:nq], rcp[:, 0:nq], ALU.mult)

    def attn_p_blocks(blocks):
        nb = len(blocks)
        ptb = [carve_at(PT_OFF, (2, 256), BF16), carve_at(S1_OFF, (2, 256), BF16), carve_at(S1_OFF + 1024, (2, 256), BF16)]
        rc = [carve_at(S1_OFF + 4096, (256,), F32), carve_at(S1_OFF + 5120, (256,), F32)]
        SB = (4, 5, 2)

        def emit_S(b):
            qv, keys, dst = blocks[b]
            psS = pst(SB[b % 3], (2, 256))
            for i in range(2):
                mm(psS[:, i, :], [(keys[i][0], qv)])
            act(ptb[b % 3][:], psS[:], AF.Exp, scale=SCALE)

        def emit_PV(b):
            qv, keys, dst = blocks[b]
            psO = pst(6 if b % 2 == 0 else 0)
            psD = pst(7 if b % 2 == 0 else 1)
            for i in range(2):
                mm(psO[:, 0:256], [(keys[i][1], ptb[b % 3][:, i, :])], start=(i == 0), stop=(i == 1))
                mm(psD[:, 0:256], [(onesb[:], ptb[b % 3][:, i, :])], start=(i == 0), stop=(i == 1))
            recip(rc[b % 2][:], psD[:, 0:256])
            tt("dve", dst, psO[:, 0:256], rc[b % 2][:], ALU.mult)
        emit_S(0)
        if nb > 1:
            emit_S(1)
        for b in range(nb):
            if b + 2 < nb:
                emit_S(b + 2)
            emit_PV(b)

    def kv_out_fm(src_f32, l, dr, hcol, ncolstot, h):
        ps = pst(7, (4, 128))
        for j in range(4):
            mm(ps[:, j, :], [(src_f32[:, j * 128:(j + 1) * 128], identf[:])])
        o = carve_at(OST_OFF + 0, (4, 128), F32)
        copy_any(o[:], ps[:])
        for q_ in range(2):
            dst = dr[2 * h + q_, l, :, hcol:hcol + 128].rearrange("(t p) d -> p t d", p=128)
            P.dma("sp", dst, o[:, 2 * q_:2 * q_ + 2, :], "ost0", is_output=True)

    def v_out_tm(ps, ncols, l, dr, col0, tti, n):
        o = carve_at(OST_OFF + 2048 + (n % 2) * 1024, (256,), F32)
        copy_any(o[:, 0:ncols], ps[:, 0:ncols])
        seq = tti // 2
        r0 = (tti % 2) * 128
        P.dma("sp", dr[seq, l, r0:r0 + 128, col0:col0 + ncols], o[:, 0:ncols], "ost%d" % (1 + n % 2), is_output=True)

    def unit_A_steps(grp, l, hp):
        qT = carve_at(QKV_OFF, (2, T), BF16)
        kT = carve_at(QKV_OFF + 4096, (2, T), BF16)
        v = carve_at(QKV_OFF + 8192, (8, 256), BF16)
        vodd = carve_at(QKV_OFF + 12288, (7, 256), BF16)
        kcT = carve_at(QKV_OFF + 15872, (2, 256), BF16)
        vcx = carve_at(QKV_OFF + 16896, (2, 256), BF16)
        knf = carve_at(OST_OFF + 2048, (512,), F32)

        def load1(slot):
            sl = slab(slot, (NK, 2, 256))
            for pi_, c_ in enumerate([QA0 + 256 * hp, KA0 + 256 * hp]):
                slab_dma(sl[:, :, pi_, :], win_src(l, [c_], 256), "slab%d" % slot)

        def comp1(slot):
            sl = slab(slot, (NK, 2, 256))
            for j in range(2):
                proj_fm(lambda h, j=j: qT[:, j, h * 512:(h + 1) * 512], sl, 0, j * 128, qkn[:, l, 0:1])
                if grp == "P":
                    raise AssertionError("use comp1P")
                else:
                    proj_fm(lambda h, j=j: kT[:, j, h * 512:(h + 1) * 512], sl, 1, j * 128, qkn[:, l, 1:2])
            pipe_flush()
        def comp1P(slot):
            sl = slab(slot, (NK, 2, 256))
            for j in range(2):
                proj_fm(lambda h, j=j: qT[:, j, h * 512:(h + 1) * 512], sl, 0, j * 128, qkn[:, l, 0:1])
                for h in range(2):
                    hs = slice(h * 512, (h + 1) * 512)
                    ps = pst(pbank())
                    i2 = nrm_cnt[0] % 2
                    post = norm_stages(ps, i2, qkn[:, l, 1:2], kT[:, j, hs], f32_dst=knf[:],
                                       after=(lambda h=h, j=j: kv_out_fm(knf, l, nka_d, (2 * hp + j) * 128, 512, h)))
                    pipe_mm(ps[:], [(sl[:, k, 1, j * 128:(j + 1) * 128], hT[:, k, hs]) for k in range(NK)], post)
            pipe_flush()

        def load2(slot):
            sl = slab(slot, (NK, 256))
            slab_dma(sl[:], win_src(l, [VA0 + 256 * hp], 256), "slab%d" % slot)

        def comp2(slot):
            sl = slab(slot, (NK, 256))
            cnt = [0]

            def vout(tti, ps):
                v_out_tm(ps, 256, l, nva_d, 256 * hp, tti, cnt[0])
                cnt[0] += 1
            proj_tm(lambda k: sl[:, k, :], 256, lambda tti: v[:, tti, :], out_fn=(vout if grp == "P" else None))
            if grp == "S":
                proj_tm(lambda k: sl[:, k, :], 256, lambda tti: vodd[:, tti, :], tiles=range(7), tok_off=64)
            pipe_flush()
            if grp == "S":
                for tc in range(2):
                    stg = carve_at(OST_OFF, (256,), F32)
                    P.dma("sp", stg[:], cka_d[l, tc * 128:(tc + 1) * 128, 256 * hp:256 * hp + 256], "ost0")
                    ps = pst(3, (2, 128))
                    for j in range(2):
                        mm(ps[:, j, :], [(stg[:, j * 128:(j + 1) * 128], identf[:])])
                    copy_any(kcT[:, :, tc * 128:(tc + 1) * 128], ps[:])
                    stg2 = carve_at(OST_OFF + 1024, (256,), F32)
                    P.dma("sp", stg2[:], cva_d[l, tc * 128:(tc + 1) * 128, 256 * hp:256 * hp + 256], "ost1")
                    copy_any(vcx[:, tc, :], stg2[:])
            if grp == "P":
                blks = []
                for j in range(2):
                    hd = 2 * hp + j
                    for s in range(4):
                        keys = [(kT[:, j, s * 256 + i * 128: s * 256 + (i + 1) * 128], v[:, 2 * s + i, j * 128:(j + 1) * 128])
                                for i in range(2)]
                        blks.append((qT[:, j, s * 256:(s + 1) * 256], keys, mixT[:, hd, s * 256:(s + 1) * 256]))
                attn_p_blocks(blks)
            else:
                for j in range(2):
                    na_head(l, 2 * hp + j, j, qT, kT, v, vodd, kcT, vcx)

        add_step(load1, comp1P if grp == "P" else comp1)
        add_step(load2, comp2)

    mb2 = carve_at(MBG_OFF, (15, 64), F32)

    def na_head(l, hd, j, qT, kT, v, vodd, kcT, vcx):
        P.dma("sp", mb2[0:64, :, :], mbf_d[l, hd].rearrange("j k c -> k j c"), "mb2")
        P.dma("sp", mb2[64:128, 0:14, :], mbf_d[l, hd, 1:15].rearrange("j k c -> k j c"), "mb2")
        P.op("dve", lambda e: e.memset(mb2[64:128, 14, :].ap, 0.0), writes=[mb2[64:128, 14, :]])
        wv = wmask[:]
        wm_b = View(bass.AP(wv.ap.tensor, wv.ap.offset, [list(wv.ap.ap[0]), [0, 15], list(wv.ap.ap[1])]), "arena", wv.ranges)
        tt("dve", mb2[:], mb2[:], wm_b, ALU.add)
        def emit_S(r):
            rs = min(max(r - 4, 0), 8)
            psS = pst(4 + (r % 2), (8, 64))
            qv = qT[:, j, r * 64:(r + 1) * 64]
            for i in range(4):
                k0 = (rs + 2 * i) * 64
                mm(psS[:, i, :], [(kT[:, j, k0:k0 + 128], qv)])
            for i in range(2):
                mm(psS[:, 4 + i, :], [(kcT[:, j, i * 128:(i + 1) * 128], qv)])
            j0 = rs - r + 7
            tS = carve_at(S1_OFF + (r % 2) * 2048, (8, 64), F32)
            pT = carve_at(PT_OFF + (r % 2) * 1024, (8, 64), BF16)
            stt("dve", tS[:, 0:4, :], psS[:, 0:4, :], SCALE, mb2[:, j0:j0 + 7:2, :], ALU.mult, ALU.add)
            act(pT[:, 0:4, :], tS[:, 0:4, :], AF.Exp)
            act(pT[:, 4:6, :], psS[:, 4:6, :], AF.Exp, scale=SCALE)

        def emit_PV(r):
            rs = min(max(r - 4, 0), 8)
            pT = carve_at(PT_OFF + (r % 2) * 1024, (8, 64), BF16)
            psO = pst(6 if r % 2 == 0 else 0)
            psD = pst(7 if r % 2 == 0 else 1)
            for i in range(6):
                if i < 4:
                    kr = rs + 2 * i
                    vv = v[:, kr // 2, j * 128:(j + 1) * 128] if kr % 2 == 0 else vodd[:, (kr - 1) // 2, j * 128:(j + 1) * 128]
                else:
                    vv = vcx[:, i - 4, j * 128:(j + 1) * 128]
                mm(psO[:, 0:64], [(vv, pT[:, i, :])], start=(i == 0), stop=(i == 5))
                mm(psD[:, 0:64], [(onesb[:], pT[:, i, :])], start=(i == 0), stop=(i == 5))
            rc = carve_at(S1_OFF + 4096 + (r % 2) * 256, (64,), F32)
            recip(rc[:], psD[:, 0:64])
            tt("dve", mixT[:, hd, r * 64:(r + 1) * 64], psO[:, 0:64], rc[:], ALU.mult)
        emit_S(0)
        for r in range(16):
            if r + 1 < 16:
                emit_S(r + 1)
            emit_PV(r)

    ropeC = carve_at(MBG_OFF, (T,), BF16)
    ropeS = carve_at(MBG_OFF + 2048, (T,), BF16)

    def rope_apply(dst, h):
        hs = slice(h * 512, (h + 1) * 512)
        psr = pst(7)
        mm(psr[:], [(permb[:], dst)])
        rtmp = carve_at(S1_OFF + 4096, (512,), F32)
        tt("dve", rtmp[:], psr[:], ropeS[:, hs], ALU.mult)
        stt("dve", dst, dst, 1.0, ropeC[:, hs], ALU.mult, ALU.mult)
        tt("dve", dst, dst, rtmp[:], ALU.add)

    def unit_B_steps(grp, l, g):
        qT = carve_at(QKV_OFF, (4, T), BF16)
        kT = carve_at(QKV_OFF + 8192, (1280,), BF16)
        v = carve_at(QKV_OFF + 10752, (10, 128), BF16)
        knf = carve_at(OST_OFF + 2048, (512,), F32)

        def load1(slot):
            sl = slab(slot, (NK, 1, 512))
            slab_dma(sl[:, :, 0, :], win_src(l, [QB0 + 512 * g], 512), "slab%d" % slot)

        def comp1(slot):
            sl = slab(slot, (NK, 1, 512))
            if grp == "S":
                P.dma("pool", ropeC[:], ropeC_d, "rope")
                P.dma("pool", ropeS[:], ropeS_d, "rope")
            for j in range(4):
                aft = (lambda h, j=j: rope_apply(qT[:, j, h * 512:(h + 1) * 512], h)) if grp == "S" else None
                proj_fm(lambda h, j=j: qT[:, j, h * 512:(h + 1) * 512], sl, 0, j * 128, qkn[:, l, 2:3], after=aft)
            pipe_flush()

        def load2(slot):
            sl = slab(slot, (NK, 2, 128))
            for pi_, c_ in enumerate([KB0 + 128 * g, VB0 + 128 * g]):
                slab_dma(sl[:, :, pi_, :], win_src(l, [c_], 128), "slab%d" % slot)

        def comp2(slot):
            sl = slab(slot, (NK, 2, 128))
            for h in range(2):
                hs = slice(h * 512, (h + 1) * 512)
                ps = pst(pbank())
                i2 = nrm_cnt[0] % 2
                if grp == "P":
                    post = norm_stages(ps, i2, qkn[:, l, 3:4], kT[:, hs], f32_dst=knf[:],
                                       after=(lambda h=h: kv_out_fm(knf, l, nkb_d, g * 128, 256, h)))
                else:
                    post = norm_stages(ps, i2, qkn[:, l, 3:4], kT[:, hs],
                                       after=(lambda h=h, hs=hs: rope_apply(kT[:, hs], h)))
                pipe_mm(ps[:], [(sl[:, k, 0, :], hT[:, k, hs]) for k in range(NK)], post)
            cnt = [0]

            def vout(tti, ps):
                v_out_tm(ps, 128, l, nvb_d, 128 * g, tti, cnt[0])
                cnt[0] += 1
            proj_tm(lambda k: sl[:, k, 1, :], 128, lambda tti: v[:, tti, :], out_fn=(vout if grp == "P" else None))
            pipe_flush()
            if grp == "S":
                for tc in range(2):
                    stg = carve_at(OST_OFF, (128,), F32)
                    P.dma("sp", stg[:], ckb_d[l, tc * 128:(tc + 1) * 128, 128 * g:128 * g + 128], "ost0")
                    ps = pst(3)
                    mm(ps[:, 0:128], [(stg[:], identf[:])])
                    copy_any(kT[:, 1024 + tc * 128:1024 + (tc + 1) * 128], ps[:, 0:128])
                    stg2 = carve_at(OST_OFF + 1024, (128,), F32)
                    P.dma("sp", stg2[:], cvb_d[l, tc * 128:(tc + 1) * 128, 128 * g:128 * g + 128], "ost1")
                    copy_any(v[:, 8 + tc, :], stg2[:])
            if grp == "P":
                blks = []
                for j in range(4):
                    hd = 4 * g + j
                    for s in range(4):
                        keys = [(kT[:, s * 256 + i * 128: s * 256 + (i + 1) * 128], v[:, 2 * s + i, :]) for i in range(2)]
                        blks.append((qT[:, j, s * 256:(s + 1) * 256], keys, mixT[:, 4 + hd, s * 256:(s + 1) * 256]))
                attn_p_blocks(blks)
            else:
                for j in range(4):
                    hd = 4 * g + j
                    keys = [(kT[:, i * 128:(i + 1) * 128], v[:, i, :]) for i in range(10)]
                    for h in range(2):
                        attn_softmax(qT[:, j, h * 512:(h + 1) * 512], 512, keys, mixT[:, 4 + hd, h * 512:(h + 1) * 512])

        add_step(load1, comp1)
        add_step(load2, comp2)

    Gts = [carve_at(MBG_OFF, (1920,), BF16), carve_at(OST_OFF, (1920,), BF16)]

    def build_G(l, hd, Gt):
        lgf = lgam[:, l * 8 + hd: l * 8 + hd + 1]
        lgb = lgam[:, l * 8 + 4 + hd: l * 8 + 4 + hd + 1]
        for c in range(3):
            cs = slice(c * 640, (c + 1) * 640)
            ta = carve_at(S1_OFF, (640,), F32)
            tb = carve_at(S1_OFF + 2560, (640,), F32)
            P.dma("sp", ta[:], xpm_d[:, cs], "gta")
            P.dma("sp", tb[:], xnm_d[:, cs], "gtb")
            act(ta[:], ta[:], AF.Exp, scale=lgf, bias=lnscale[:])
            act(tb[:], tb[:], AF.Exp, scale=lgb, bias=lnscale[:])
            tt("dve", Gt[:, cs], ta[:], tb[:], ALU.add)

    epsv = carve_at(PAR_OFF + 4008, (1,), F32)
    P.op("dve", lambda e: e.memset(epsv[:].ap, EPS), writes=[epsv[:]])
    lnscale = carve_at(PAR_OFF + 4000, (1,), F32)
    P.op("dve", lambda e: e.memset(lnscale[:].ap, float(np.log(SCALE))), writes=[lnscale[:]])

    def unit_C_steps(grp, l, hp):
        qT = carve_at(QKV_OFF, (2, T), BF16)
        kT = carve_at(QKV_OFF + 4096, (2, T), BF16)
        v = carve_at(QKV_OFF + 8192, (8, 256), BF16)
        gT = carve_at(QKV_OFF + 12288, (2, T), BF16)
        kw = carve_at(QKV_OFF + 16384, (2, 2, 128), BF16)
        r0 = carve_at(QKV_OFF + 17408, (2, 128), BF16)
        wfb = carve_at(QKV_OFF + 17920, (2, 512), BF16)

        def load1(slot):
            sl = slab(slot, (NK, 2, 256))
            for pi_, c_ in enumerate([QC0 + 256 * hp, KC0 + 256 * hp]):
                slab_dma(sl[:, :, pi_, :], win_src(l, [c_], 256), "slab%d" % slot)

        def comp1(slot):
            sl = slab(slot, (NK, 2, 256))
            for j in range(2):
                build_G(l, 2 * hp + j, Gts[j])
            for j in range(2):
                proj_fm(lambda h, j=j: qT[:, j, h * 512:(h + 1) * 512], sl, 0, j * 128, None)
                proj_fm(lambda h, j=j: kT[:, j, h * 512:(h + 1) * 512], sl, 1, j * 128, None)
            pipe_flush()

        def load2(slot):
            sl = slab(slot, (NK, 2, 256))
            for pi_, c_ in enumerate([VC0 + 256 * hp, GC0 + 256 * hp]):
                slab_dma(sl[:, :, pi_, :], win_src(l, [c_], 256), "slab%d" % slot)

        def comp2(slot):
            sl = slab(slot, (NK, 2, 256))
            proj_tm(lambda k: sl[:, k, 0, :], 256, lambda tti: v[:, tti, :])
            for j in range(2):
                for h in range(2):
                    hs = slice(h * 512, (h + 1) * 512)
                    ps = pst(pbank())

                    def post(ps=ps, j=j, hs=hs):
                        act(gT[:, j, hs], ps[:], AF.Silu)
                    pipe_mm(ps[:], [(sl[:, k, 1, j * 128:(j + 1) * 128], hT[:, k, hs]) for k in range(NK)], post)
            pipe_flush()
            blkc = [0]
            for j in range(2):
                hd = 2 * hp + j
                Gt = Gts[j]
                lgf = lgam[:, l * 8 + hd: l * 8 + hd + 1]
                lgb = lgam[:, l * 8 + 4 + hd: l * 8 + 4 + hd + 1]
                gain = retgT[:, l, hd:hd + 1]
                if grp == "P":
                    act(wcol[:, 0:2], posT[:, 0:2], AF.Exp, scale=lgf, bias=lnscale[:])
                    act(wcol[:, 2:4], posT[:, 2:4], AF.Exp, scale=lgb, bias=lnscale[:])
                    blocks = [(s * 256, 256, [(s * 256 + i * 128) for i in range(2)]) for s in range(4)]
                else:
                    blocks = [(h * 512, 512, [i * 128 for i in range(8)]) for h in range(2)]
                    for dr in range(2):
                        stg = carve_at(QKV_OFF + 16384, (128,), F32)
                        P.dma("sp", stg[:], sret_d[l, dr, hd], "r0stg")
                        copy_any(r0[:, dr, :], stg[:])
                for (t0, nq, ktiles) in blocks:
                    psO = pst(6 + (blkc[0] % 2))
                    blkc[0] += 1
                    nk_ = len(ktiles)
                    nmm = nk_ + (2 if grp == "S" else 0)

                    def emit_S(i, t0=t0, nq=nq, ktiles=ktiles, j=j, Gt=Gt):
                        s0 = ktiles[i]
                        psS = pst(4 + (i % 2))
                        mm(psS[:, 0:nq], [(kT[:, j, s0:s0 + 128], qT[:, j, t0:t0 + nq])])
                        m0 = t0 - s0 + 896
                        tt("dve", pt[:, i % 2, 0:nq], psS[:, 0:nq], Gt[:, m0:m0 + nq], ALU.mult)

                    def emit_PV(i, nq=nq, ktiles=ktiles, j=j, psO=psO, nmm=nmm):
                        s0 = ktiles[i]
                        mm(psO[:, 0:nq], [(v[:, s0 // 128, j * 128:(j + 1) * 128], pt[:, i % 2, 0:nq])],
                           start=(i == 0), stop=(i == nmm - 1))
                    emit_S(0)
                    for i in range(nk_):
                        if i + 1 < nk_:
                            emit_S(i + 1)
                        emit_PV(i)
                    n = nk_
                    if grp == "S":
                        for dr, tab, lg in ((0, t1f_d, lgf), (1, t1b_d, lgb)):
                            tw = carve_at(S1_OFF + 5120 + 0, (512,), F32)
                            P.dma("sp", tw[:], tab[:, t0:t0 + 512], "tw")
                            act(wfb[:, dr, :], tw[:], AF.Exp, scale=lg)
                            tt("dve", wfb[:, dr, :], wfb[:, dr, :], qT[:, j, t0:t0 + 512], ALU.mult)
                            mm(psO[:, 0:nq], [(r0[:, dr, :], wfb[:, dr, :])], start=(n == 0), stop=(n == nmm - 1))
                            n += 1
                    i2 = nrm_cnt[0] % 2
                    nrm_cnt[0] += 1
                    act(sq2[:, i2, 0:nq], psO[:, 0:nq], AF.Square)
                    psn = pst(3)
                    mm(psn[:, 0:nq], [(onesb[:], sq2[:, i2, 0:nq])])
                    act(rcp[:, 0:nq], psn[:, 0:nq], AF.Ln, scale=1.0 / 128, bias=epsv[:])
                    act(rcp[:, 0:nq], rcp[:, 0:nq], AF.Exp, scale=-0.5)
                    stt("dve", rcp[:, 0:nq], psO[:, 0:nq], gain, rcp[:, 0:nq], ALU.mult, ALU.mult)
                    tt("dve", mixT[:, 12 + hd, t0:t0 + nq], rcp[:, 0:nq], gT[:, j, t0:t0 + nq], ALU.mult)
                if grp == "P":
                    for s in range(4):
                        for i in range(2):
                            pk = pst(3)
                            mm(pk[:, 0:128], [(kT[:, j, s * 256 + i * 128: s * 256 + (i + 1) * 128], identb[:])])
                            ts("dve", kw[:, 0, i, :], pk[:, 0:128], wcol[:, i:i + 1], None, ALU.mult)
                            ts("dve", kw[:, 1, i, :], pk[:, 0:128], wcol[:, 2 + i:3 + i], None, ALU.mult)
                        for dr in range(2):
                            pss = pst(dr)
                            mm(pss[:, 0:128], [(kw[:, dr, i, :], v[:, 2 * s + i, j * 128:(j + 1) * 128]) for i in range(2)])
                            o = carve_at(QKV_OFF + 17920 + dr * 512, (128,), F32)
                            copy_any(o[:], pss[:, 0:128])
                            P.dma("sp", nst_d[s, l, dr, hd], o[:], "ostc%d" % dr, is_output=True)

        add_step(load1, comp1)
        add_step(load2, comp2)

    def outproj_steps(l, cnd):
        for s in range(4):
            def load(slot, s=s):
                sl = slab(slot, (NK, 512))
                src = wout_d[l].rearrange("(k p) c -> p k c", p=128)[:, :, s * 512:(s + 1) * 512]
                slab_dma(sl[:], src, "slab%d" % slot)

            def comp(slot, s=s):
                sl = slab(slot, (NK, 512))
                for j in range(4):
                    oc = 4 * s + j
                    for h in range(2):
                        hs = slice(h * 512, (h + 1) * 512)
                        ps = pst(pbank())

                        def post(ps=ps, oc=oc, hs=hs):
                            stt("dve", xT[:, oc, hs], ps[:], modA[:, l, cnd, 2, oc:oc + 1], xT[:, oc, hs], ALU.mult, ALU.add)
                        pipe_mm(ps[:], [(sl[:, k, j * 128:(j + 1) * 128], mixT[:, k, hs]) for k in range(NK)], post)
                if s == 3:
                    pipe_flush()
            add_step(load, comp)

    def ffn_steps(grp, l, cnd):
        nseq = 4 if grp == "P" else 1
        seqlen = T // nseq
        ngrp = (NH + 3) // 4
        aT = [carve_at(MIX_OFF + i * 8192, (4, T), BF16) for i in range(2)]
        uoffs = [MIX_OFF + 16384, MIX_OFF + 16384 + 4160, MIX_OFF + 16384 + 8320, MBG_OFF]
        upad = [[carve_at(uoffs[i * 2 + gv], (nseq, seqlen + 2), F32) for gv in range(2)] for i in range(2)]
        def zero_pads(slot):
            for i in range(2):
                for gv in range(2):
                    u = upad[i][gv]
                    P.op("dve", lambda e, u=u: e.memset(u[:, :, 0:1].ap, 0.0), writes=[u[:, :, 0:1]])
                    P.op("dve", lambda e, u=u: e.memset(u[:, :, seqlen + 1:seqlen + 2].ap, 0.0),
                         writes=[u[:, :, seqlen + 1:seqlen + 2]])
        add_step(None, zero_pads, ring=(0, 1, 2))
        cvt = [carve_at(S1_OFF + i * 4096, (nseq, seqlen), F32) for i in range(2)]
        ccount = [0]

        u_groups = []
        d_list = []
        for gi in range(ngrp):
            u_list = []
            c0 = gi * 4
            ncg = min(4, NH - c0)
            pairs = [(c0 + 2 * i, min(2, ncg - 2 * i)) for i in range((ncg + 1) // 2)]
            for pi, (cc0, ncp) in enumerate(pairs):
                def loadU(slot, cc0=cc0, ncp=ncp):
                    sl = slab(slot, (NK, 2, 256))
                    base = wup_d[l].rearrange("(k p) c -> p k c", p=128)
                    for gv in range(2):
                        col = gv * FFN + cc0 * 128
                        slab_dma(sl[:, :, gv, 0:ncp * 128], base[:, :, col:col + ncp * 128], "slab%d" % slot)

                def compU(slot, cc0=cc0, ncp=ncp, gi=gi, c0=c0):
                    sl = slab(slot, (NK, 2, 256))
                    for ci in range(ncp):
                        c = cc0 + ci
                        ib = ccount[0] % 2
                        ccount[0] += 1
                        for gv in range(2):
                            u = upad[ib][gv]
                            bp = 2 * (nrm_cnt[0] % 2)
                            nrm_cnt[0] += 1
                            mm_multi([(pst(bp + h)[:], [(sl[:, k, gv, ci * 128:(ci + 1) * 128], hT[:, k, h * 512:(h + 1) * 512])
                                                         for k in range(NK)]) for h in range(2)])
                            for h in range(2):
                                if nseq == 4:
                                    act(u[:, 2 * h:2 * h + 2, 1:seqlen + 1], pst(bp + h, (2, 256))[:], AF.Copy)
                                else:
                                    act(u[:, 0, 1 + h * 512:1 + (h + 1) * 512], pst(bp + h)[:], AF.Copy)
                            cb = convbT[:, l, gv * NH + c: gv * NH + c + 1]
                            t = cvt[gv]
                            act(t[:], u[:, :, 1:seqlen + 1], AF.Identity, scale=convwT[:, l, 1, gv * NH + c: gv * NH + c + 1], bias=cb)
                            stt("dve", t[:], u[:, :, 0:seqlen], convwT[:, l, 0, gv * NH + c: gv * NH + c + 1], t[:], ALU.mult, ALU.add)
                            stt("dve", t[:], u[:, :, 2:seqlen + 2], convwT[:, l, 2, gv * NH + c: gv * NH + c + 1], t[:], ALU.mult, ALU.add)
                        act(cvt[0][:], cvt[0][:], AF.Silu)
                        dst = aT[gi % 2][:, c - c0, :]
                        tt("pool", View(dst.ap.rearrange("p (s t) -> p s t", s=nseq), dst.name, dst.ranges), cvt[0][:], cvt[1][:], ALU.mult)
                u_list.append((loadU, compU))

            def loadD(slot, c0=c0, ncg=ncg):
                sl = slab(slot, (4, D))
                src = wdn_d[l][c0 * 128:(c0 + ncg) * 128, :].rearrange("(k p) c -> p k c", p=128)
                slab_dma(sl[:, 0:ncg, :], src, "slab%d" % slot)

            def compD(slot, c0=c0, ncg=ncg, gi=gi):
                sl = slab(slot, (4, D))
                a = aT[gi % 2]
                for oc in range(NK):
                    bp = 4 + 2 * (nrm_cnt[0] % 2)
                    nrm_cnt[0] += 1
                    mm_multi([(pst(bp + h)[:], [(sl[:, k, oc * 128:(oc + 1) * 128], a[:, k, h * 512:(h + 1) * 512])
                                                 for k in range(ncg)]) for h in range(2)])
                    for h in range(2):
                        hs = slice(h * 512, (h + 1) * 512)
                        stt("dve", xT[:, oc, hs], pst(bp + h)[:], modA[:, l, cnd, 5, oc:oc + 1], xT[:, oc, hs], ALU.mult, ALU.add)
            d_list.append((loadD, compD))
            u_groups.append(u_list)

        for gi in range(ngrp):
            ul = u_groups[gi]
            add_step(ul[0][0], ul[0][1], ring=(0, 1, 2))
            if gi > 0:
                add_step(d_list[gi - 1][0], d_list[gi - 1][1], ring=(0, 1, 2))
            for (ld_, cp_) in ul[1:]:
                add_step(ld_, cp_, ring=(0, 1, 2))
        add_step(d_list[ngrp - 1][0], d_list[ngrp - 1][1], ring=(0, 1, 2))

    for grp in groups:
        cnd = 0 if grp == "P" else 1
        load_x_steps(xp_d if grp == "P" else xs_d)
        for l in layers:
            if do_mix:
                bg_require(l, 1)
                norm_mod_steps(l, cnd, 0, 1)
                bg_on[0] = True
                for hp in range(2):
                    unit_A_steps(grp, l, hp)
                for g in range(2):
                    unit_B_steps(grp, l, g)
                for hp in range(2):
                    unit_C_steps(grp, l, hp)
                bg_on[0] = False
                bg_require(l, 2)
                bg_on[0] = True
                outproj_steps(l, cnd)
                bg_on[0] = False
            if do_ffn:
                bg_require(l, 4)
                norm_mod_steps(l, cnd, 3, 4)
                bg_require(l, 5, ring=(0, 1, 2))
                bg_on[0] = True
                ffn_steps(grp, l, cnd)
                bg_on[0] = False
        store_y_steps(yp_d if grp == "P" else ys_d)

    LOOK = 2
    slot_last = {0: -3, 1: -2, 2: -1}
    step_slot = {}
    widx = [i for i, s in enumerate(steps) if s[0] is not None]
    nxt = [0]

    def ensure_loaded(upto_w, cur_i):
        while nxt[0] < len(widx) and nxt[0] <= upto_w:
            si = widx[nxt[0]]
            ld, _, ring = steps[si]
            allowed = [s_ for s_ in ring if slot_last[s_] < cur_i]
            if not allowed:
                break
            slot = min(allowed, key=lambda s_: slot_last[s_])
            slot_last[slot] = si
            step_slot[si] = slot
            ld(slot)
            nxt[0] += 1

    wpos = 0
    for i, (ld, comp, ring) in enumerate(steps):
        while wpos < len(widx) and widx[wpos] <= i:
            wpos += 1
        look = LOOK if len(ring) == 3 else 1
        ensure_loaded(wpos - 1 + look, i)
        assert ld is None or i in step_slot, i
        comp(step_slot.get(i))
        pipe_flush()

    P.finish()

    with nc.Block() as block:
        @block.tensor
        def _(e):
            P.replay("pe", e)

        @block.scalar
        def _(e):
            P.replay("act", e)

        @block.vector
        def _(e):
            P.replay("dve", e)

        @block.gpsimd
        def _(e):
            P.replay("pool", e)

        @block.sync
        def _(e):
            P.replay("sp", e)
    stack.close()
    return nc, P


def _consts():
    c = {}
    c["identf"] = np.eye(128, dtype=np.float32)
    perm = np.zeros((128, 128), np.float32)
    for m in range(128):
        blk = m // 32
        src = m + 32 if blk % 2 == 0 else m - 32
        perm[src, m] = 1.0
    c["perm"] = perm
    nf = 32
    inv_freq = np.power(np.float32(10000.0), -np.arange(nf, dtype=np.float32) / nf).astype(np.float32)
    t = np.arange(T)
    rows = (t // 64).astype(np.float32)
    cols = (t % 64).astype(np.float32)
    ang_r = rows[None, :] * inv_freq[:, None]
    ang_c = cols[None, :] * inv_freq[:, None]
    C = np.concatenate([np.cos(ang_r), np.cos(ang_r), np.cos(ang_c), np.cos(ang_c)], 0)
    S = np.concatenate([-np.sin(ang_r), np.sin(ang_r), -np.sin(ang_c), np.sin(ang_c)], 0)
    c["ropeC"] = C.astype(np.float32)
    c["ropeS"] = S.astype(np.float32)
    kc = np.arange(64)[:, None]
    qc = np.arange(64)[None, :]
    cs = np.clip(qc - 8, 0, 48)
    inw = (kc >= cs) & (kc < cs + 16)
    wm = np.where(inw, 0.0, -1e30).astype(np.float32)
    c["wmask"] = np.concatenate([wm, wm], 0)
    p = np.arange(128)[:, None]
    m = np.arange(1920)[None, :]
    X = (m - 896 - p).astype(np.float32)
    c["xpm"] = np.where(X >= 0, X, 1e9).astype(np.float32)
    c["xnm"] = np.where(X <= 0, -X, 1e9).astype(np.float32)
    tt_ = np.arange(T, dtype=np.float32)[None, :]
    c["t1f"] = np.broadcast_to(tt_ + 1.0, (128, T)).astype(np.float32).copy()
    c["t1b"] = np.broadcast_to(T - tt_, (128, T)).astype(np.float32).copy()
    pp = np.arange(128, dtype=np.float32)
    c["pos"] = np.stack([255 - pp, 127 - pp, pp, 128 + pp], 1).astype(np.float32)
    return c


def _pm(v, n):
    v = np.asarray(v, np.float32)
    lead = v.shape[:-1]
    r = v.reshape(lead + (n, 128))
    return np.ascontiguousarray(np.moveaxis(r, -1, 0))


_CACHE = {}


def kernel(x_prompt, x_sample, cache_ka, cache_va, cache_kb, cache_vb, state_ret, c, c_ctx,
           ada_w, ada_b, norm_mix_g, norm_ffn_g, w_in, q_norm_a, k_norm_a, q_norm_b, k_norm_b,
           na_rel_bias, ret_decay_logit, ret_norm_g, w_out, w_up, conv_w, conv_b, w_down, _cfg=None):
    f = lambda a: np.ascontiguousarray(np.asarray(a, dtype=np.float32))
    key = repr(_cfg)
    if key not in _CACHE:
        _CACHE[key] = build_program(_cfg)
    nc, _ = _CACHE[key]
    consts = _consts()
    shared = {
        "ada_w": f(ada_w), "w_in": f(w_in), "w_out": f(w_out), "w_up": f(w_up), "w_down": f(w_down),
        "adabT": _pm(ada_b, 96).reshape(128, -1),
        "gmixT": _pm(norm_mix_g, 16).reshape(128, -1),
        "gffnT": _pm(norm_ffn_g, 16).reshape(128, -1),
        "qkn": np.ascontiguousarray(np.stack([f(q_norm_a), f(k_norm_a), f(q_norm_b), f(k_norm_b)], -1).transpose(1, 0, 2)).reshape(128, -1),
        "retgT": _pm(ret_norm_g, 4).reshape(128, -1),
        "convwT": _pm(conv_w, 86).reshape(128, -1),
        "convbT": _pm(conv_b, 86).reshape(128, -1),
        "dlog": np.ascontiguousarray(np.broadcast_to(f(ret_decay_logit).reshape(1, 16), (128, 16))),
    }
    rb = f(na_rel_bias)
    dcidx = np.clip(np.arange(64)[:, None] - np.arange(64)[None, :] + 15, 0, 30)
    shared["mbf"] = np.ascontiguousarray(rb[:, :, :, dcidx])
    shared.update(consts)
    xpf, xsf = f(x_prompt), f(x_sample)
    cka, cva, ckb, cvb, sr = f(cache_ka), f(cache_va), f(cache_kb), f(cache_vb), f(state_ret)
    cf, ccf = f(c), f(c_ctx)
    in_maps = []
    for core in range(8):
        b = core // 2
        cond = np.stack([ccf, cf[b]], 0)
        m = dict(shared)
        m["xp"] = xpf[4 * core:4 * core + 4].reshape(T, D)
        m["xs"] = xsf[b]
        m["condT"] = np.ascontiguousarray(cond.reshape(2, 16, 128).transpose(2, 1, 0)).reshape(128, 32)
        m["cka"] = cka[b].reshape(2, 256, 512)
        m["cva"] = cva[b].reshape(2, 256, 512)
        m["ckb"] = ckb[b].reshape(2, 256, 256)
        m["cvb"] = cvb[b].reshape(2, 256, 256)
        m["sret"] = sr[b]
        in_maps.append(m)
    res = run_bass_kernel_spmd(nc, in_maps, core_ids=list(range(8)))
    R = res.results
    y_prompt = np.concatenate([R[i]["yp"].reshape(4, 256, D) for i in range(8)], 0)
    y_sample = np.stack([R[2 * b]["ys"] for b in range(4)], 0)
    nka = np.concatenate([R[i]["nka"].reshape(4, 2, 256, 4, 128) for i in range(8)], 0)
    nva = np.concatenate([R[i]["nva"].reshape(4, 2, 256, 4, 128) for i in range(8)], 0)
    nkb = np.concatenate([R[i]["nkb"].reshape(4, 2, 256, 2, 128) for i in range(8)], 0)
    nvb = np.concatenate([R[i]["nvb"].reshape(4, 2, 256, 2, 128) for i in range(8)], 0)
    nst = np.concatenate([R[i]["nst"] for i in range(8)], 0)
    return (y_prompt.astype(np.float32), y_sample.astype(np.float32), nka.astype(np.float32), nva.astype(np.float32),
            nkb.astype(np.float32), nvb.astype(np.float32), nst.astype(np.float32))
```

```python
import numpy as np
from contextlib import ExitStack
import concourse.bass as bass
import concourse.mybir as mybir
from concourse.bass_utils import run_bass_kernel_spmd

F32 = mybir.dt.float32
BF16 = mybir.dt.bfloat16
AF = mybir.ActivationFunctionType
ALU = mybir.AluOpType

D = 2048
T = 1024
NK = 16
IN_DIM = 5120
FFN = 5504
NH = 43
EPS = 1e-6
SCALE = 128.0 ** -0.5
SEM_EPOCH = 6000

QA0, KA0, VA0, QB0, KB0, VB0, QC0, KC0, VC0, GC0 = 0, 512, 1024, 1536, 2560, 2816, 3072, 3584, 4096, 4608


class View:
    __slots__ = ("ap", "name", "ranges")

    def __init__(self, ap, name, ranges):
        self.ap = ap
        self.name = name
        self.ranges = ranges


class Tn:
    def __init__(self, name, base_ap, fshape, esize, base_byte):
        self.name = name
        self.base_ap = base_ap
        self.fshape = tuple(fshape)
        self.esize = esize
        self.base_byte = base_byte
        st = []
        s = 1
        for n in reversed(self.fshape):
            st.append(s)
            s *= n
        self.fstrides = tuple(reversed(st))

    def __getitem__(self, idx):
        if not isinstance(idx, tuple):
            idx = (idx,)
        ap = self.base_ap[idx]
        fidx = list(idx[1:]) + [slice(None)] * (len(self.fshape) - (len(idx) - 1))
        sel = []
        for i, n in zip(fidx, self.fshape):
            if isinstance(i, slice):
                a, b, c = i.indices(n)
                sel.append(list(range(a, b, c)))
            else:
                sel.append([i])
        last = sel[-1]
        lo_l, hi_l = last[0], last[-1] + 1
        bases = [0]
        for d in range(len(sel) - 1):
            bases = [b + j * self.fstrides[d] for b in bases for j in sel[d]]
        rs = []
        for b in bases:
            lo = self.base_byte + (b + lo_l) * self.esize
            hi = self.base_byte + (b + hi_l) * self.esize
            if rs and rs[-1][1] == lo:
                rs[-1] = (rs[-1][0], hi)
            else:
                rs.append((lo, hi))
        return View(ap, self.name, rs)


class Prog:
    ENG = ("pe", "act", "dve", "pool", "sp")

    def __init__(self, nc, stack):
        self.nc = nc
        self.stack = stack
        self.entries = {e: [] for e in self.ENG}
        self.count = {e: 0 for e in self.ENG}
        self.epoch = {e: 0 for e in self.ENG}
        self.sems = {}
        self.acc = {}
        self.seen = {e: {} for e in self.ENG}
        self.dma_cnt = {}
        self.out_tokens = {}
        self.n_ops = 0

    def sem(self, key):
        s = self.sems.get(key)
        if s is None:
            s = self.stack.enter_context(self.nc.semaphore("s%d" % len(self.sems)))
            self.sems[key] = s
        return s

    def _deps(self, reads, writes):
        waits = {}
        for v in reads:
            lst = self.acc.get(v.name)
            if not lst:
                continue
            for (lo, hi) in v.ranges:
                for e in lst:
                    if e[3] and e[0] < hi and lo < e[1]:
                        k, c = e[2]
                        if waits.get(k, 0) < c:
                            waits[k] = c
        for v in writes:
            lst = self.acc.get(v.name)
            if not lst:
                continue
            for (lo, hi) in v.ranges:
                for e in lst:
                    if e[0] < hi and lo < e[1]:
                        k, c = e[2]
                        if waits.get(k, 0) < c:
                            waits[k] = c
        return waits

    def _record(self, reads, writes, tok):
        for v in writes:
            lst = self.acc.setdefault(v.name, [])
            for (lo, hi) in v.ranges:
                lst[:] = [e for e in lst if not (lo <= e[0] and e[1] <= hi)]
                lst.append((lo, hi, tok, True))
        for v in reads:
            lst = self.acc.setdefault(v.name, [])
            for (lo, hi) in v.ranges:
                k = tok[0]
                if k[0] == "E":
                    lst[:] = [e for e in lst if not (e[0] == lo and e[1] == hi and (not e[3]) and e[2][0] == k)]
                lst.append((lo, hi, tok, False))

    def _filter(self, eng, waits):
        out = []
        seen = self.seen[eng]
        for k, c in waits.items():
            if k[0] == "E":
                if eng == "pe" and k[1] == "pe":
                    continue
                mx = seen.get(("EP", k[1]), -1)
                if k[2] < mx:
                    continue
            if seen.get(k, 0) >= c:
                continue
            seen[k] = c
            if k[0] == "E":
                seen[("EP", k[1])] = max(seen.get(("EP", k[1]), -1), k[2])
            out.append((self.sem(k), c))
        return out

    @staticmethod
    def _norm(reads, writes):
        r2 = []
        w2 = []
        for v in reads:
            if v.name.startswith("ps"):
                w2.append(View(v.ap, v.name, [(0, 2048)]))
            else:
                r2.append(v)
        for v in writes:
            if v.name.startswith("ps"):
                w2.append(View(v.ap, v.name, [(0, 2048)]))
            else:
                w2.append(v)
        return r2, w2

    def op(self, eng, fn, reads=(), writes=()):
        reads, writes = self._norm(reads, writes)
        waits = self._filter(eng, self._deps(reads, writes))
        if self.count[eng] >= SEM_EPOCH:
            self.epoch[eng] += 1
            self.count[eng] = 0
        self.count[eng] += 1
        key = ("E", eng, self.epoch[eng])
        tok = (key, self.count[eng])
        self._record(reads, writes, tok)
        self.entries[eng].append((waits, fn, self.sem(key), 1))
        self.n_ops += 1
        return tok

    def dma(self, q, out, in_, key, is_output=False):
        reads = [in_] if isinstance(in_, View) else []
        writes = [out] if isinstance(out, View) else []
        waits = self._filter(q, self._deps(reads, writes))
        k = ("D", key)
        self.dma_cnt[k] = self.dma_cnt.get(k, 0) + 16
        tok = (k, self.dma_cnt[k])
        self._record(reads, writes, tok)
        oap = out.ap if isinstance(out, View) else out
        iap = in_.ap if isinstance(in_, View) else in_
        self.entries[q].append((waits, lambda e: e.dma_start(out=oap, in_=iap), self.sem(k), 16))
        if is_output:
            self.out_tokens[k] = self.dma_cnt[k]
        self.n_ops += 1
        return tok

    def finish(self):
        waits = [(self.sem(k), c) for k, c in self.out_tokens.items()]
        self.entries["sp"].append((waits, None, None, 0))

    def replay(self, eng, e):
        for waits, fn, sem, inc in self.entries[eng]:
            for s, c in waits:
                e.wait_ge(s, c)
            if fn is None:
                continue
            ins = fn(e)
            ins.then_inc(sem, inc)


def build_program(cfg=None):
    cfg = cfg or {}
    groups = cfg.get("groups", ("P", "S"))
    layers = cfg.get("layers", (0, 1))
    do_ffn = cfg.get("ffn", True)
    do_mix = cfg.get("mix", True)

    nc = bass.Bass("TRN2", target_bir_lowering=False)
    dram = {}

    def din(name, shape):
        dram[name] = nc.dram_tensor(name, list(shape), F32, kind="ExternalInput")
        return dram[name].ap()

    def dout(name, shape):
        dram[name] = nc.dram_tensor(name, list(shape), F32, kind="ExternalOutput")
        return dram[name].ap()

    xp_d = din("xp", [T, D])
    xs_d = din("xs", [T, D])
    condT_d = din("condT", [128, NK * 2])
    cka_d = din("cka", [2, 256, 512])
    cva_d = din("cva", [2, 256, 512])
    ckb_d = din("ckb", [2, 256, 256])
    cvb_d = din("cvb", [2, 256, 256])
    sret_d = din("sret", [2, 2, 4, 128, 128])
    adaw_d = din("ada_w", [2, D, 6 * D])
    win_d = din("w_in", [2, D, IN_DIM])
    wout_d = din("w_out", [2, D, D])
    wup_d = din("w_up", [2, D, 2 * FFN])
    wdn_d = din("w_down", [2, FFN, D])
    adabT_d = din("adabT", [128, 2 * 96])
    gmixT_d = din("gmixT", [128, 2 * 16])
    gffnT_d = din("gffnT", [128, 2 * 16])
    qkn_d = din("qkn", [128, 8])
    retgT_d = din("retgT", [128, 8])
    convwT_d = din("convwT", [128, 2 * 3 * 86])
    convbT_d = din("convbT", [128, 2 * 86])
    dlog_d = din("dlog", [128, 16])
    mbf_d = din("mbf", [2, 4, 15, 64, 64])
    identf_d = din("identf", [128, 128])
    perm_d = din("perm", [128, 128])
    ropeC_d = din("ropeC", [128, T])
    ropeS_d = din("ropeS", [128, T])
    wmask_d = din("wmask", [128, 64])
    xpm_d = din("xpm", [128, 1920])
    xnm_d = din("xnm", [128, 1920])
    t1f_d = din("t1f", [128, T])
    t1b_d = din("t1b", [128, T])
    pos_d = din("pos", [128, 4])

    yp_d = dout("yp", [T, D])
    ys_d = dout("ys", [T, D])
    nka_d = dout("nka", [4, 2, 256, 512])
    nva_d = dout("nva", [4, 2, 256, 512])
    nkb_d = dout("nkb", [4, 2, 256, 256])
    nvb_d = dout("nvb", [4, 2, 256, 256])
    nst_d = dout("nst", [4, 2, 2, 4, 128, 128])

    stack = ExitStack()
    ARENA_BYTES = 211968
    arena = stack.enter_context(nc.sbuf_tensor("arena", [128, ARENA_BYTES // 2], BF16))
    psb = [stack.enter_context(nc.psum_tensor("psb%d" % i, [128, 512], F32)) for i in range(8)]
    P = Prog(nc, stack)

    carve_off = [0]

    def carve_at(off, fshape, dt):
        es = 4 if dt == F32 else 2
        n = 1
        for s in fshape:
            n *= s
        nb = n * es
        assert off % 4 == 0 and off + nb <= ARENA_BYTES, (off, nb)
        ap = arena[:, off // 2: (off + nb) // 2]
        if dt == F32:
            ap = ap.bitcast(F32)
        if len(fshape) == 2:
            ap = ap.rearrange("p (a b) -> p a b", a=fshape[0])
        elif len(fshape) == 3:
            ap = ap.rearrange("p (a b c) -> p a b c", a=fshape[0], b=fshape[1])
        elif len(fshape) == 4:
            ap = ap.rearrange("p (a b c d) -> p a b c d", a=fshape[0], b=fshape[1], c=fshape[2])
        return Tn("arena", ap, fshape, es, off)

    def carve(nbytes):
        off = carve_off[0]
        nbytes = (nbytes + 63) // 64 * 64
        carve_off[0] += nbytes
        assert carve_off[0] <= ARENA_BYTES, carve_off[0]
        return off

    def pst(b, fshape=(512,)):
        ap = psb[b][:, 0:int(np.prod(fshape))]
        if len(fshape) == 2:
            ap = ap.rearrange("p (a b) -> p a b", a=fshape[0])
        elif len(fshape) == 3:
            ap = ap.rearrange("p (a b c) -> p a b c", a=fshape[0], b=fshape[1])
        return Tn("ps%d" % b, ap, fshape, 4, 0)

    XT_OFF = carve(65536)
    HT_OFF = carve(32768)
    MIX_OFF = carve(32768)
    SLAB_OFF = [carve(16384), carve(16384), None]
    QKV_OFF = carve(20480)
    SLAB_OFF[2] = QKV_OFF
    S1_OFF = carve(8192)
    PT_OFF = carve(2048)
    MBG_OFF = carve(4096)
    OST_OFF = carve(4096)
    CONST_OFF = carve(1280)
    MODV_OFF = carve(1536 + 1536 + 768)
    PAR_OFF = carve(4096)

    xT = carve_at(XT_OFF, (NK, T), F32)
    hT = carve_at(HT_OFF, (NK, T), BF16)
    mixT = carve_at(MIX_OFF, (NK, T), BF16)
    identb = carve_at(CONST_OFF, (128,), BF16)
    onesb = carve_at(CONST_OFF + 256, (128,), BF16)
    identf = carve_at(CONST_OFF + 512, (128,), F32)
    permb = carve_at(CONST_OFF + 1024, (128,), BF16)
    modT = carve_at(MODV_OFF, (2, 96, 2), F32)
    modA = carve_at(MODV_OFF + 1536, (2, 2, 6, 16), F32)
    adabT = carve_at(MODV_OFF + 3072, (2, 96), F32)
    po = PAR_OFF
    convwT = carve_at(po, (2, 3, 86), F32); po += 2064
    convbT = carve_at(po, (2, 86), F32); po += 688
    gmixT = carve_at(po, (2, 16), F32); po += 128
    gffnT = carve_at(po, (2, 16), F32); po += 128
    qkn = carve_at(po, (2, 4), F32); po += 32
    retgT = carve_at(po, (2, 4), F32); po += 32
    dlog = carve_at(po, (16,), F32); po += 64
    lgam = carve_at(po, (16,), F32); po += 64
    wcol = carve_at(po, (4,), F32); po += 16
    posT = carve_at(po, (4,), F32); po += 16
    scT = carve_at(po, (NK, 2), BF16); po += 64
    condT = carve_at(po, (NK, 2), F32); po += 128
    wmask = carve_at(po, (64,), F32); po += 256
    assert po <= PAR_OFF + 4096, po

    def slab_dma(dst, src, key, nsplit=2):
        nk_ = dst.ap.shape[1]
        assert src.shape[1] == nk_, (dst.ap.shape, src.shape)
        bounds = [round(i_ * nk_ / nsplit) for i_ in range(nsplit + 1)]
        for a_, b_ in zip(bounds[:-1], bounds[1:]):
            if b_ > a_:
                P.dma("pool", View(dst.ap[:, a_:b_], dst.name, dst.ranges), src[:, a_:b_], key)

    def slab(slot, fshape, dt=BF16):
        return carve_at(SLAB_OFF[slot], fshape, dt)

    def mm(out, pairs, start=True, stop=True):
        n = len(pairs)

        def fn(e):
            ins = None
            for i, (l, r) in enumerate(pairs):
                ins = e.matmul(out.ap, lhsT=l.ap, rhs=r.ap, start=(start and i == 0), stop=(stop and i == n - 1))
            return ins
        rd = []
        for l, r in pairs:
            rd.append(l)
            rd.append(r)
            assert tuple(out.ap.shape[1:]) == tuple(r.ap.shape[1:]) or int(np.prod(out.ap.shape[1:])) == int(np.prod(r.ap.shape[1:])), (out.ap.shape, l.ap.shape, r.ap.shape)
            assert out.ap.shape[0] == l.ap.shape[-1] or len(l.ap.shape) > 2, (out.ap.shape, l.ap.shape, r.ap.shape)
        return P.op("pe", fn, reads=rd, writes=[out])

    def act(out, in_, func, scale=1.0, bias=0.0, eng="act"):
        rd = [in_]
        sc = scale
        bi = bias
        if isinstance(scale, View):
            rd.append(scale)
            sc = scale.ap
        if isinstance(bias, View):
            rd.append(bias)
            bi = bias.ap
        return P.op("act", lambda e: e.activation(out=out.ap, in_=in_.ap, func=func, scale=sc, bias=bi),
                    reads=rd, writes=[out])

    def tcopy(eng, out, in_):
        return P.op(eng, lambda e: e.tensor_copy(out=out.ap, in_=in_.ap), reads=[in_], writes=[out])

    def tt(eng, out, in0, in1, op):
        return P.op(eng, lambda e: e.tensor_tensor(out=out.ap, in0=in0.ap, in1=in1.ap, op=op),
                    reads=[in0, in1], writes=[out])

    def ts(eng, out, in0, s1, s2, op0, op1=None):
        rd = [in0]
        a1 = s1
        a2 = s2
        if isinstance(s1, View):
            rd.append(s1)
            a1 = s1.ap
        if isinstance(s2, View):
            rd.append(s2)
            a2 = s2.ap
        if op1 is None:
            return P.op(eng, lambda e: e.tensor_scalar(out=out.ap, in0=in0.ap, scalar1=a1, scalar2=None, op0=op0),
                        reads=rd, writes=[out])
        return P.op(eng, lambda e: e.tensor_scalar(out=out.ap, in0=in0.ap, scalar1=a1, scalar2=a2, op0=op0, op1=op1),
                    reads=rd, writes=[out])

    def stt(eng, out, in0, sc, in1, op0, op1):
        rd = [in0, in1]
        a = sc
        if isinstance(sc, View):
            rd.append(sc)
            a = sc.ap
        return P.op(eng, lambda e: e.scalar_tensor_tensor(out=out.ap, in0=in0.ap, scalar=a, in1=in1.ap, op0=op0, op1=op1),
                    reads=rd, writes=[out])

    def recip(out, in_):
        return P.op("dve", lambda e: e.reciprocal(out=out.ap, in_=in_.ap), reads=[in_], writes=[out])

    nsmall = [0]

    def load_small(tn, dap, key=None):
        nsmall[0] += 1
        P.dma("sp", tn[:], dap, key or ("small%d" % nsmall[0]))

    alt = [0]

    def evac_eng():
        alt[0] ^= 1
        return "dve" if alt[0] else "act"

    def copy_any(out, in_):
        if evac_eng() == "dve":
            tcopy("dve", out, in_)
        else:
            act(out, in_, AF.Copy)

    tmpc = carve_at(S1_OFF, (128,), F32)
    P.dma("sp", tmpc[:], identf_d, "smallA")
    tcopy("dve", identb[:], tmpc[:])
    tcopy("dve", identf[:], tmpc[:])
    tmpc2 = carve_at(S1_OFF + 512, (128,), F32)
    P.dma("sp", tmpc2[:], perm_d, "smallB")
    tcopy("dve", permb[:], tmpc2[:])
    P.op("dve", lambda e: e.memset(onesb[:].ap, 1.0), writes=[onesb[:]])
    load_small(adabT, adabT_d.rearrange("p (a b) -> p a b", a=2))
    load_small(convwT, convwT_d.rearrange("p (a b c) -> p a b c", a=2, b=3))
    load_small(convbT, convbT_d.rearrange("p (a b) -> p a b", a=2))
    load_small(gmixT, gmixT_d.rearrange("p (a b) -> p a b", a=2))
    load_small(gffnT, gffnT_d.rearrange("p (a b) -> p a b", a=2))
    load_small(qkn, qkn_d.rearrange("p (a b) -> p a b", a=2))
    load_small(retgT, retgT_d.rearrange("p (a b) -> p a b", a=2))
    load_small(dlog, dlog_d)
    load_small(posT, pos_d)
    load_small(condT, condT_d.rearrange("p (a b) -> p a b", a=NK))
    load_small(wmask, wmask_d)
    act(lgam[:], dlog[:], AF.Exp, scale=-1.0)
    onev = carve_at(PAR_OFF + 4016, (1,), F32)
    P.op("dve", lambda e: e.memset(onev[:].ap, 1.0), writes=[onev[:]])
    act(lgam[:], lgam[:], AF.Ln, bias=onev[:])
    ts("dve", lgam[:], lgam[:], -1.0, None, ALU.mult)
    act(scT[:], condT[:], AF.Silu)

    steps = []

    def add_step(load, compute, ring=(0, 1)):
        steps.append((load, compute, ring))
        if bg_on[0]:
            bg_emit_one(ring)

    def ada_slab(l, s):
        def load(slot):
            sl = slab(slot, (NK, 512))
            src = adaw_d[l].rearrange("(k p) c -> p k c", p=128)[:, :, s * 512:(s + 1) * 512]
            slab_dma(sl[:], src, "slab%d" % slot)

        def comp(slot):
            sl = slab(slot, (NK, 512))
            ps = pst(s % 3, (4, 128))
            for j in range(4):
                mm(ps[:, j, 0:2], [(sl[:, k, j * 128:(j + 1) * 128], scT[:, k, :]) for k in range(NK)])
            for j in range(4):
                ts("dve", modT[:, l, 4 * s + j, :], ps[:, j, 0:2], adabT[:, l, 4 * s + j: 4 * s + j + 1], None, ALU.add)
        return load, comp

    def ada_fin(l, kinds):
        for cnd in range(2):
            for (kind, src_blk, g) in ((0, 1, gmixT), (3, 4, gffnT)):
                if kind in kinds:
                    stt("dve", modA[:, l, cnd, kind, :], modT[:, l, src_blk * 16:(src_blk + 1) * 16, cnd], 1.0,
                        g[:, l, :], ALU.add, ALU.mult)
            for (kind, src_blk) in ((1, 0), (2, 2), (4, 3), (5, 5)):
                if kind in kinds:
                    tcopy("dve", modA[:, l, cnd, kind, :], modT[:, l, src_blk * 16:(src_blk + 1) * 16, cnd])

    bg = []
    for l_ in layers:
        for (lo_, hi_, kinds_) in ((0, 8, (0, 1)), (8, 12, (2,)), (12, 20, (3, 4)), (20, 24, (5,))):
            for s_ in range(lo_, hi_):
                bg.append(("slab", l_, s_))
            bg.append(("fin", l_, kinds_))
    bg_pos = [0]
    bg_on = [False]

    def bg_emit_one(ring):
        if bg_pos[0] >= len(bg):
            return
        item = bg[bg_pos[0]]
        bg_pos[0] += 1
        if item[0] == "slab":
            ld, cp = ada_slab(item[1], item[2])
            steps.append((ld, cp, ring))
        else:
            steps.append((None, (lambda slot, it=item: ada_fin(it[1], it[2])), ring))

    def bg_require(l, kind, ring=(0, 1)):
        tgt = None
        for i_, it in enumerate(bg):
            if it[0] == "fin" and it[1] == l and kind in it[2]:
                tgt = i_
        while bg_pos[0] <= tgt:
            bg_emit_one(ring)

    def load_x_steps(x_d):
        def comp(slot):
            for tti in range(8):
                stg = carve_at(MIX_OFF + (tti % 2) * 8192, (D,), F32)
                P.dma("sp", stg[:], x_d[tti * 128:(tti + 1) * 128, :], "xstg%d" % (tti % 2))
                for g in range(4):
                    ps = pst((tti * 4 + g) % 3, (4, 128))
                    for j in range(4):
                        k = 4 * g + j
                        mm(ps[:, j, :], [(stg[:, k * 128:(k + 1) * 128], identf[:])])
                    copy_any(xT[:, 4 * g:4 * g + 4, tti * 128:(tti + 1) * 128], ps[:])
        add_step(None, comp)

    def store_y_steps(y_d):
        def comp(slot):
            n = 0
            for tti in range(8):
                for g in range(4):
                    ps = pst(n % 3, (4, 128))
                    for j in range(4):
                        k = 4 * g + j
                        mm(ps[:, j, :], [(xT[:, k, tti * 128:(tti + 1) * 128], identf[:])])
                    stg = carve_at(MIX_OFF + (n % 4) * 2048, (512,), F32)
                    copy_any(stg[:], pst(n % 3)[:])
                    P.dma("sp", y_d[tti * 128:(tti + 1) * 128, g * 512:(g + 1) * 512], stg[:], "ystg%d" % (n % 4), is_output=True)
                    n += 1
        add_step(None, comp)

    rstd = carve_at(S1_OFF, (T,), F32)
    ntmp = carve_at(S1_OFF + 4096, (2, 512), F32)
    sqb = carve_at(PT_OFF, (2, 512), BF16)

    def norm_mod_steps(l, cnd, ka, kb_):
        def comp(slot):
            for h in range(2):
                hs = slice(h * 512, (h + 1) * 512)
                psn = pst(3)
                for k in range(NK):
                    act(sqb[:, k % 2, :], xT[:, k, hs], AF.Square)
                    mm(psn[:], [(onesb[:], sqb[:, k % 2, :])], start=(k == 0), stop=(k == NK - 1))
                act(rstd[:, hs], psn[:], AF.Ln, scale=1.0 / D, bias=epsv[:])
                act(rstd[:, hs], rstd[:, hs], AF.Exp, scale=-0.5)
                for k in range(NK):
                    stt("dve", ntmp[:, k % 2, :], xT[:, k, hs], modA[:, l, cnd, ka, k:k + 1], rstd[:, hs], ALU.mult, ALU.mult)
                    act(hT[:, k, hs], ntmp[:, k % 2, :], AF.Identity, bias=modA[:, l, cnd, kb_, k:k + 1])
        add_step(None, comp)

    def win_src(l, pieces, w):
        base = win_d[l].rearrange("(k p) c -> p k c", p=128)
        if len(pieces) == 1:
            return base[:, :, pieces[0]:pieces[0] + w]
        step = pieces[1] - pieces[0]
        for i in range(len(pieces) - 1):
            assert pieces[i + 1] - pieces[i] == step
        b0 = base[:, :, pieces[0]:pieces[0] + w]
        a = b0.ap
        return bass.AP(b0.tensor, b0.offset, [list(a[0]), list(a[1]), [step, len(pieces)], list(a[2])])

    tmpS = carve_at(S1_OFF, (2, 512), F32)
    rcp = carve_at(S1_OFF + 4096, (512,), F32)
    sq2 = carve_at(S1_OFF + 6144, (2, 512), BF16)
    pt = carve_at(PT_OFF, (2, 512), BF16)
    ostg = carve_at(OST_OFF, (2, 512), F32)
    nrm_cnt = [0]

    pending = []
    PROJ_BANKS = (0, 1, 2, 4, 5)

    def pbank():
        b = PROJ_BANKS[nrm_cnt[0] % len(PROJ_BANKS)]
        nrm_cnt[0] += 1
        return b

    def pipe_mm(ps_view, pairs, post):
        mm(ps_view, pairs)
        stages = list(post) if isinstance(post, (list, tuple)) else [post]
        for ent in list(pending):
            ent.pop(0)()
            if not ent:
                pending.remove(ent)
        pending.append(stages)

    def pipe_flush():
        while pending:
            for ent in list(pending):
                ent.pop(0)()
                if not ent:
                    pending.remove(ent)

    def norm_stages(ps, i2, gvec, dst, f32_dst=None, after=None):
        psn = pst(3 if i2 == 0 else 6)

        def stA():
            act(sq2[:, i2, :], ps[:], AF.Square)
            mm(psn[:], [(onesb[:], sq2[:, i2, :])])

        def stB():
            act(tmpS[:, i2, :], psn[:], AF.Ln, scale=1.0 / 128, bias=epsv[:])
            act(tmpS[:, i2, :], tmpS[:, i2, :], AF.Exp, scale=-0.5)
            if f32_dst is not None:
                stt("dve", f32_dst, ps[:], gvec, tmpS[:, i2, :], ALU.mult, ALU.mult)
                copy_any(dst, f32_dst)
            else:
                stt("dve", dst, ps[:], gvec, tmpS[:, i2, :], ALU.mult, ALU.mult)
            if after is not None:
                after()
        return [stA, stB]

    def proj_fm(dst_fn, sl, piece, ncol0, gvec, after=None):
        for h in range(2):
            hs = slice(h * 512, (h + 1) * 512)
            ps = pst(pbank())
            i2 = nrm_cnt[0] % 2
            aft = (lambda h=h: after(h)) if after is not None else None
            if gvec is None:
                def post(ps=ps, h=h, aft=aft):
                    copy_any(dst_fn(h), ps[:])
                    if aft is not None:
                        aft()
            else:
                post = norm_stages(ps, i2, gvec, dst_fn(h), after=aft)
            pipe_mm(ps[:], [(sl[:, k, piece, ncol0:ncol0 + 128], hT[:, k, hs]) for k in range(NK)], post)

    def proj_tm(sl_fn, ncols, dst_fn, tiles=range(8), tok_off=0, out_fn=None):
        for tti in tiles:
            ps = pst(pbank())
            t0 = tok_off + tti * 128

            def post(ps=ps, tti=tti):
                copy_any(dst_fn(tti), ps[:, 0:ncols])
                if out_fn is not None:
                    out_fn(tti, ps)
            pipe_mm(ps[:, 0:ncols], [(hT[:, k, t0:t0 + 128], sl_fn(k)) for k in range(NK)], post)

    ablk = [0]

    def attn_softmax(qv, nq, keys, dst, bias_fn=None):
        nkt = len(keys)
        sel = ablk[0] % 2
        ablk[0] += 1
        psO = pst(6 if sel == 0 else 0)
        psD = pst(7 if sel == 0 else 1)

        def emit_S(i):
            kv, vv = keys[i]
            psS = pst(4 + (i % 2))
            mm(psS[:, 0:nq], [(kv, qv)])
            act(pt[:, i % 2, 0:nq], psS[:, 0:nq], AF.Exp, scale=SCALE)

        def emit_PV(i):
            kv, vv = keys[i]
            mm(psO[:, 0:nq], [(vv, pt[:, i % 2, 0:nq])], start=(i == 0), stop=(i == nkt - 1))
            mm(psD[:, 0:nq], [(onesb[:], pt[:, i % 2, 0:nq])], start=(i == 0), stop=(i == nkt - 1))
        emit_S(0)
        for i in range(nkt):
            if i + 1 < nkt:
                emit_S(i + 1)
            emit_PV(i)
        recip(rcp[:, 0:nq], psD[:, 0:nq])
        tt("dve", dst, psO[:, 0:nq], rcp[:, 0:nq], ALU.mult)

    def attn_p_blocks(blocks):
        nb = len(blocks)
        ptb = [carve_at(PT_OFF, (2, 256), BF16), carve_at(S1_OFF, (2, 256), BF16), carve_at(S1_OFF + 1024, (2, 256), BF16)]
        rc = [carve_at(S1_OFF + 4096, (256,), F32), carve_at(S1_OFF + 5120, (256,), F32)]
        SB = (4, 5, 2)

        def emit_S(b):
            qv, keys, dst = blocks[b]
            psS = pst(SB[b % 3], (2, 256))
            for i in range(2):
                mm(psS[:, i, :], [(keys[i][0], qv)])
            act(ptb[b % 3][:], psS[:], AF.Exp, scale=SCALE)

        def emit_PV(b):
            qv, keys, dst = blocks[b]
            psO = pst(6 if b % 2 == 0 else 0)
            psD = pst(7 if b % 2 == 0 else 1)
            for i in range(2):
                mm(psO[:, 0:256], [(keys[i][1], ptb[b % 3][:, i, :])], start=(i == 0), stop=(i == 1))
                mm(psD[:, 0:256], [(onesb[:], ptb[b % 3][:, i, :])], start=(i == 0), stop=(i == 1))
            recip(rc[b % 2][:], psD[:, 0:256])
            tt("dve", dst, psO[:, 0:256], rc[b % 2][:], ALU.mult)
        emit_S(0)
        if nb > 1:
            emit_S(1)
        for b in range(nb):
            if b + 2 < nb:
                emit_S(b + 2)
            emit_PV(b)

    def kv_out_fm(src_f32, l, dr, hcol, ncolstot, h):
        ps = pst(7, (4, 128))
        for j in range(4):
            mm(ps[:, j, :], [(src_f32[:, j * 128:(j + 1) * 128], identf[:])])
        o = carve_at(OST_OFF + 0, (4, 128), F32)
        copy_any(o[:], ps[:])
        for q_ in range(2):
            dst = dr[2 * h + q_, l, :, hcol:hcol + 128].rearrange("(t p) d -> p t d", p=128)
            P.dma("sp", dst, o[:, 2 * q_:2 * q_ + 2, :], "ost0", is_output=True)

    def v_out_tm(ps, ncols, l, dr, col0, tti, n):
        o = carve_at(OST_OFF + 2048 + (n % 2) * 1024, (256,), F32)
        copy_any(o[:, 0:ncols], ps[:, 0:ncols])
        seq = tti // 2
        r0 = (tti % 2) * 128
        P.dma("sp", dr[seq, l, r0:r0 + 128, col0:col0 + ncols], o[:, 0:ncols], "ost%d" % (1 + n % 2), is_output=True)

    def unit_A_steps(grp, l, hp):
        qT = carve_at(QKV_OFF, (2, T), BF16)
        kT = carve_at(QKV_OFF + 4096, (2, T), BF16)
        v = carve_at(QKV_OFF + 8192, (8, 256), BF16)
        vodd = carve_at(QKV_OFF + 12288, (7, 256), BF16)
        kcT = carve_at(QKV_OFF + 15872, (2, 256), BF16)
        vcx = carve_at(QKV_OFF + 16896, (2, 256), BF16)
        knf = carve_at(OST_OFF + 2048, (512,), F32)

        def load1(slot):
            sl = slab(slot, (NK, 2, 256))
            for pi_, c_ in enumerate([QA0 + 256 * hp, KA0 + 256 * hp]):
                slab_dma(sl[:, :, pi_, :], win_src(l, [c_], 256), "slab%d" % slot)

        def comp1(slot):
            sl = slab(slot, (NK, 2, 256))
            for j in range(2):
                proj_fm(lambda h, j=j: qT[:, j, h * 512:(h + 1) * 512], sl, 0, j * 128, qkn[:, l, 0:1])
                if grp == "P":
                    raise AssertionError("use comp1P")
                else:
                    proj_fm(lambda h, j=j: kT[:, j, h * 512:(h + 1) * 512], sl, 1, j * 128, qkn[:, l, 1:2])
            pipe_flush()
        def comp1P(slot):
            sl = slab(slot, (NK, 2, 256))
            for j in range(2):
                proj_fm(lambda h, j=j: qT[:, j, h * 512:(h + 1) * 512], sl, 0, j * 128, qkn[:, l, 0:1])
                for h in range(2):
                    hs = slice(h * 512, (h + 1) * 512)
                    ps = pst(pbank())
                    i2 = nrm_cnt[0] % 2
                    post = norm_stages(ps, i2, qkn[:, l, 1:2], kT[:, j, hs], f32_dst=knf[:],
                                       after=(lambda h=h, j=j: kv_out_fm(knf, l, nka_d, (2 * hp + j) * 128, 512, h)))
                    pipe_mm(ps[:], [(sl[:, k, 1, j * 128:(j + 1) * 128], hT[:, k, hs]) for k in range(NK)], post)
            pipe_flush()

        def load2(slot):
            sl = slab(slot, (NK, 256))
            slab_dma(sl[:], win_src(l, [VA0 + 256 * hp], 256), "slab%d" % slot)

        def comp2(slot):
            sl = slab(slot, (NK, 256))
            cnt = [0]

            def vout(tti, ps):
                v_out_tm(ps, 256, l, nva_d, 256 * hp, tti, cnt[0])
                cnt[0] += 1
            proj_tm(lambda k: sl[:, k, :], 256, lambda tti: v[:, tti, :], out_fn=(vout if grp == "P" else None))
            if grp == "S":
                proj_tm(lambda k: sl[:, k, :], 256, lambda tti: vodd[:, tti, :], tiles=range(7), tok_off=64)
            pipe_flush()
            if grp == "S":
                for tc in range(2):
                    stg = carve_at(OST_OFF, (256,), F32)
                    P.dma("sp", stg[:], cka_d[l, tc * 128:(tc + 1) * 128, 256 * hp:256 * hp + 256], "ost0")
                    ps = pst(3, (2, 128))
                    for j in range(2):
                        mm(ps[:, j, :], [(stg[:, j * 128:(j + 1) * 128], identf[:])])
                    copy_any(kcT[:, :, tc * 128:(tc + 1) * 128], ps[:])
                    stg2 = carve_at(OST_OFF + 1024, (256,), F32)
                    P.dma("sp", stg2[:], cva_d[l, tc * 128:(tc + 1) * 128, 256 * hp:256 * hp + 256], "ost1")
                    copy_any(vcx[:, tc, :], stg2[:])
            if grp == "P":
                blks = []
                for j in range(2):
                    hd = 2 * hp + j
                    for s in range(4):
                        keys = [(kT[:, j, s * 256 + i * 128: s * 256 + (i + 1) * 128], v[:, 2 * s + i, j * 128:(j + 1) * 128])
                                for i in range(2)]
                        blks.append((qT[:, j, s * 256:(s + 1) * 256], keys, mixT[:, hd, s * 256:(s + 1) * 256]))
                attn_p_blocks(blks)
            else:
                for j in range(2):
                    na_head(l, 2 * hp + j, j, qT, kT, v, vodd, kcT, vcx)

        add_step(load1, comp1P if grp == "P" else comp1)
        add_step(load2, comp2)

    mb2 = carve_at(MBG_OFF, (15, 64), F32)

    def na_head(l, hd, j, qT, kT, v, vodd, kcT, vcx):
        P.dma("sp", mb2[0:64, :, :], mbf_d[l, hd].rearrange("j k c -> k j c"), "mb2")
        P.dma("sp", mb2[64:128, 0:14, :], mbf_d[l, hd, 1:15].rearrange("j k c -> k j c"), "mb2")
        P.op("dve", lambda e: e.memset(mb2[64:128, 14, :].ap, 0.0), writes=[mb2[64:128, 14, :]])
        wv = wmask[:]
        wm_b = View(bass.AP(wv.ap.tensor, wv.ap.offset, [list(wv.ap.ap[0]), [0, 15], list(wv.ap.ap[1])]), "arena", wv.ranges)
        tt("dve", mb2[:], mb2[:], wm_b, ALU.add)
        def emit_S(r):
            rs = min(max(r - 4, 0), 8)
            psS = pst(4 + (r % 2), (8, 64))
            qv = qT[:, j, r * 64:(r + 1) * 64]
            for i in range(4):
                k0 = (rs + 2 * i) * 64
                mm(psS[:, i, :], [(kT[:, j, k0:k0 + 128], qv)])
            for i in range(2):
                mm(psS[:, 4 + i, :], [(kcT[:, j, i * 128:(i + 1) * 128], qv)])
            j0 = rs - r + 7
            tS = carve_at(S1_OFF + (r % 2) * 2048, (8, 64), F32)
            pT = carve_at(PT_OFF + (r % 2) * 1024, (8, 64), BF16)
            stt("dve", tS[:, 0:4, :], psS[:, 0:4, :], SCALE, mb2[:, j0:j0 + 7:2, :], ALU.mult, ALU.add)
            act(pT[:, 0:4, :], tS[:, 0:4, :], AF.Exp)
            act(pT[:, 4:6, :], psS[:, 4:6, :], AF.Exp, scale=SCALE)

        def emit_PV(r):
            rs = min(max(r - 4, 0), 8)
            pT = carve_at(PT_OFF + (r % 2) * 1024, (8, 64), BF16)
            psO = pst(6 if r % 2 == 0 else 0)
            psD = pst(7 if r % 2 == 0 else 1)
            for i in range(6):
                if i < 4:
                    kr = rs + 2 * i
                    vv = v[:, kr // 2, j * 128:(j + 1) * 128] if kr % 2 == 0 else vodd[:, (kr - 1) // 2, j * 128:(j + 1) * 128]
                else:
                    vv = vcx[:, i - 4, j * 128:(j + 1) * 128]
                mm(psO[:, 0:64], [(vv, pT[:, i, :])], start=(i == 0), stop=(i == 5))
                mm(psD[:, 0:64], [(onesb[:], pT[:, i, :])], start=(i == 0), stop=(i == 5))
            rc = carve_at(S1_OFF + 4096 + (r % 2) * 256, (64,), F32)
            recip(rc[:], psD[:, 0:64])
            tt("dve", mixT[:, hd, r * 64:(r + 1) * 64], psO[:, 0:64], rc[:], ALU.mult)
        emit_S(0)
        for r in range(16):
            if r + 1 < 16:
                emit_S(r + 1)
            emit_PV(r)

    ropeC = carve_at(MBG_OFF, (T,), BF16)
    ropeS = carve_at(MBG_OFF + 2048, (T,), BF16)

    def rope_apply(dst, h):
        hs = slice(h * 512, (h + 1) * 512)
        psr = pst(7)
        mm(psr[:], [(permb[:], dst)])
        rtmp = carve_at(S1_OFF + 4096, (512,), F32)
        tt("dve", rtmp[:], psr[:], ropeS[:, hs], ALU.mult)
        stt("dve", dst, dst, 1.0, ropeC[:, hs], ALU.mult, ALU.mult)
        tt("dve", dst, dst, rtmp[:], ALU.add)

    def unit_B_steps(grp, l, g):
        qT = carve_at(QKV_OFF, (4, T), BF16)
        kT = carve_at(QKV_OFF + 8192, (1280,), BF16)
        v = carve_at(QKV_OFF + 10752, (10, 128), BF16)
        knf = carve_at(OST_OFF + 2048, (512,), F32)

        def load1(slot):
            sl = slab(slot, (NK, 1, 512))
            slab_dma(sl[:, :, 0, :], win_src(l, [QB0 + 512 * g], 512), "slab%d" % slot)

        def comp1(slot):
            sl = slab(slot, (NK, 1, 512))
            if grp == "S":
                P.dma("pool", ropeC[:], ropeC_d, "rope")
                P.dma("pool", ropeS[:], ropeS_d, "rope")
            for j in range(4):
                aft = (lambda h, j=j: rope_apply(qT[:, j, h * 512:(h + 1) * 512], h)) if grp == "S" else None
                proj_fm(lambda h, j=j: qT[:, j, h * 512:(h + 1) * 512], sl, 0, j * 128, qkn[:, l, 2:3], after=aft)
            pipe_flush()

        def load2(slot):
            sl = slab(slot, (NK, 2, 128))
            for pi_, c_ in enumerate([KB0 + 128 * g, VB0 + 128 * g]):
                slab_dma(sl[:, :, pi_, :], win_src(l, [c_], 128), "slab%d" % slot)

        def comp2(slot):
            sl = slab(slot, (NK, 2, 128))
            for h in range(2):
                hs = slice(h * 512, (h + 1) * 512)
                ps = pst(pbank())
                i2 = nrm_cnt[0] % 2
                if grp == "P":
                    post = norm_stages(ps, i2, qkn[:, l, 3:4], kT[:, hs], f32_dst=knf[:],
                                       after=(lambda h=h: kv_out_fm(knf, l, nkb_d, g * 128, 256, h)))
                else:
                    post = norm_stages(ps, i2, qkn[:, l, 3:4], kT[:, hs],
                                       after=(lambda h=h, hs=hs: rope_apply(kT[:, hs], h)))
                pipe_mm(ps[:], [(sl[:, k, 0, :], hT[:, k, hs]) for k in range(NK)], post)
            cnt = [0]

            def vout(tti, ps):
                v_out_tm(ps, 128, l, nvb_d, 128 * g, tti, cnt[0])
                cnt[0] += 1
            proj_tm(lambda k: sl[:, k, 1, :], 128, lambda tti: v[:, tti, :], out_fn=(vout if grp == "P" else None))
            pipe_flush()
            if grp == "S":
                for tc in range(2):
                    stg = carve_at(OST_OFF, (128,), F32)
                    P.dma("sp", stg[:], ckb_d[l, tc * 128:(tc + 1) * 128, 128 * g:128 * g + 128], "ost0")
                    ps = pst(3)
                    mm(ps[:, 0:128], [(stg[:], identf[:])])
                    copy_any(kT[:, 1024 + tc * 128:1024 + (tc + 1) * 128], ps[:, 0:128])
                    stg2 = carve_at(OST_OFF + 1024, (128,), F32)
                    P.dma("sp", stg2[:], cvb_d[l, tc * 128:(tc + 1) * 128, 128 * g:128 * g + 128], "ost1")
                    copy_any(v[:, 8 + tc, :], stg2[:])
            if grp == "P":
                blks = []
                for j in range(4):
                    hd = 4 * g + j
                    for s in range(4):
                        keys = [(kT[:, s * 256 + i * 128: s * 256 + (i + 1) * 128], v[:, 2 * s + i, :]) for i in range(2)]
                        blks.append((qT[:, j, s * 256:(s + 1) * 256], keys, mixT[:, 4 + hd, s * 256:(s + 1) * 256]))
                attn_p_blocks(blks)
            else:
                for j in range(4):
                    hd = 4 * g + j
                    keys = [(kT[:, i * 128:(i + 1) * 128], v[:, i, :]) for i in range(10)]
                    for h in range(2):
                        attn_softmax(qT[:, j, h * 512:(h + 1) * 512], 512, keys, mixT[:, 4 + hd, h * 512:(h + 1) * 512])

        add_step(load1, comp1)
        add_step(load2, comp2)

    Gts = [carve_at(MBG_OFF, (1920,), BF16), carve_at(OST_OFF, (1920,), BF16)]

    def build_G(l, hd, Gt):
        lgf = lgam[:, l * 8 + hd: l * 8 + hd + 1]
        lgb = lgam[:, l * 8 + 4 + hd: l * 8 + 4 + hd + 1]
        for c in range(3):
            cs = slice(c * 640, (c + 1) * 640)
            ta = carve_at(S1_OFF, (640,), F32)
            tb = carve_at(S1_OFF + 2560, (640,), F32)
            P.dma("sp", ta[:], xpm_d[:, cs], "gta")
            P.dma("sp", tb[:], xnm_d[:, cs], "gtb")
            act(ta[:], ta[:], AF.Exp, scale=lgf, bias=lnscale[:])
            act(tb[:], tb[:], AF.Exp, scale=lgb, bias=lnscale[:])
            tt("dve", Gt[:, cs], ta[:], tb[:], ALU.add)

    epsv = carve_at(PAR_OFF + 4008, (1,), F32)
    P.op("dve", lambda e: e.memset(epsv[:].ap, EPS), writes=[epsv[:]])
    lnscale = carve_at(PAR_OFF + 4000, (1,), F32)
    P.op("dve", lambda e: e.memset(lnscale[:].ap, float(np.log(SCALE))), writes=[lnscale[:]])

    def unit_C_steps(grp, l, hp):
        qT = carve_at(QKV_OFF, (2, T), BF16)
        kT = carve_at(QKV_OFF + 4096, (2, T), BF16)
        v = carve_at(QKV_OFF + 8192, (8, 256), BF16)
        gT = carve_at(QKV_OFF + 12288, (2, T), BF16)
        kw = carve_at(QKV_OFF + 16384, (2, 2, 128), BF16)
        r0 = carve_at(QKV_OFF + 17408, (2, 128), BF16)
        wfb = carve_at(QKV_OFF + 17920, (2, 512), BF16)

        def load1(slot):
            sl = slab(slot, (NK, 2, 256))
            for pi_, c_ in enumerate([QC0 + 256 * hp, KC0 + 256 * hp]):
                slab_dma(sl[:, :, pi_, :], win_src(l, [c_], 256), "slab%d" % slot)

        def comp1(slot):
            sl = slab(slot, (NK, 2, 256))
            for j in range(2):
                build_G(l, 2 * hp + j, Gts[j])
            for j in range(2):
                proj_fm(lambda h, j=j: qT[:, j, h * 512:(h + 1) * 512], sl, 0, j * 128, None)
                proj_fm(lambda h, j=j: kT[:, j, h * 512:(h + 1) * 512], sl, 1, j * 128, None)
            pipe_flush()

        def load2(slot):
            sl = slab(slot, (NK, 2, 256))
            for pi_, c_ in enumerate([VC0 + 256 * hp, GC0 + 256 * hp]):
                slab_dma(sl[:, :, pi_, :], win_src(l, [c_], 256), "slab%d" % slot)

        def comp2(slot):
            sl = slab(slot, (NK, 2, 256))
            proj_tm(lambda k: sl[:, k, 0, :], 256, lambda tti: v[:, tti, :])
            for j in range(2):
                for h in range(2):
                    hs = slice(h * 512, (h + 1) * 512)
                    ps = pst(pbank())

                    def post(ps=ps, j=j, hs=hs):
                        act(gT[:, j, hs], ps[:], AF.Silu)
                    pipe_mm(ps[:], [(sl[:, k, 1, j * 128:(j + 1) * 128], hT[:, k, hs]) for k in range(NK)], post)
            pipe_flush()
            blkc = [0]
            for j in range(2):
                hd = 2 * hp + j
                Gt = Gts[j]
                lgf = lgam[:, l * 8 + hd: l * 8 + hd + 1]
                lgb = lgam[:, l * 8 + 4 + hd: l * 8 + 4 + hd + 1]
                gain = retgT[:, l, hd:hd + 1]
                if grp == "P":
                    act(wcol[:, 0:2], posT[:, 0:2], AF.Exp, scale=lgf, bias=lnscale[:])
                    act(wcol[:, 2:4], posT[:, 2:4], AF.Exp, scale=lgb, bias=lnscale[:])
                    blocks = [(s * 256, 256, [(s * 256 + i * 128) for i in range(2)]) for s in range(4)]
                else:
                    blocks = [(h * 512, 512, [i * 128 for i in range(8)]) for h in range(2)]
                    for dr in range(2):
                        stg = carve_at(QKV_OFF + 16384, (128,), F32)
                        P.dma("sp", stg[:], sret_d[l, dr, hd], "r0stg")
                        copy_any(r0[:, dr, :], stg[:])
                for (t0, nq, ktiles) in blocks:
                    psO = pst(6 + (blkc[0] % 2))
                    blkc[0] += 1
                    nk_ = len(ktiles)
                    nmm = nk_ + (2 if grp == "S" else 0)

                    def emit_S(i, t0=t0, nq=nq, ktiles=ktiles, j=j, Gt=Gt):
                        s0 = ktiles[i]
                        psS = pst(4 + (i % 2))
                        mm(psS[:, 0:nq], [(kT[:, j, s0:s0 + 128], qT[:, j, t0:t0 + nq])])
                        m0 = t0 - s0 + 896
                        tt("dve", pt[:, i % 2, 0:nq], psS[:, 0:nq], Gt[:, m0:m0 + nq], ALU.mult)

                    def emit_PV(i, nq=nq, ktiles=ktiles, j=j, psO=psO, nmm=nmm):
                        s0 = ktiles[i]
                        mm(psO[:, 0:nq], [(v[:, s0 // 128, j * 128:(j + 1) * 128], pt[:, i % 2, 0:nq])],
                           start=(i == 0), stop=(i == nmm - 1))
                    emit_S(0)
                    for i in range(nk_):
                        if i + 1 < nk_:
                            emit_S(i + 1)
                        emit_PV(i)
                    n = nk_
                    if grp == "S":
                        for dr, tab, lg in ((0, t1f_d, lgf), (1, t1b_d, lgb)):
                            tw = carve_at(S1_OFF + 5120 + 0, (512,), F32)
                            P.dma("sp", tw[:], tab[:, t0:t0 + 512], "tw")
                            act(wfb[:, dr, :], tw[:], AF.Exp, scale=lg)
                            tt("dve", wfb[:, dr, :], wfb[:, dr, :], qT[:, j, t0:t0 + 512], ALU.mult)
                            mm(psO[:, 0:nq], [(r0[:, dr, :], wfb[:, dr, :])], start=(n == 0), stop=(n == nmm - 1))
                            n += 1
                    i2 = nrm_cnt[0] % 2
                    nrm_cnt[0] += 1
                    act(sq2[:, i2, 0:nq], psO[:, 0:nq], AF.Square)
                    psn = pst(3)
                    mm(psn[:, 0:nq], [(onesb[:], sq2[:, i2, 0:nq])])
                    act(rcp[:, 0:nq], psn[:, 0:nq], AF.Ln, scale=1.0 / 128, bias=epsv[:])
                    act(rcp[:, 0:nq], rcp[:, 0:nq], AF.Exp, scale=-0.5)
                    stt("dve", rcp[:, 0:nq], psO[:, 0:nq], gain, rcp[:, 0:nq], ALU.mult, ALU.mult)
                    tt("dve", mixT[:, 12 + hd, t0:t0 + nq], rcp[:, 0:nq], gT[:, j, t0:t0 + nq], ALU.mult)
                if grp == "P":
                    for s in range(4):
                        for i in range(2):
                            pk = pst(3)
                            mm(pk[:, 0:128], [(kT[:, j, s * 256 + i * 128: s * 256 + (i + 1) * 128], identb[:])])
                            ts("dve", kw[:, 0, i, :], pk[:, 0:128], wcol[:, i:i + 1], None, ALU.mult)
                            ts("dve", kw[:, 1, i, :], pk[:, 0:128], wcol[:, 2 + i:3 + i], None, ALU.mult)
                        for dr in range(2):
                            pss = pst(dr)
                            mm(pss[:, 0:128], [(kw[:, dr, i, :], v[:, 2 * s + i, j * 128:(j + 1) * 128]) for i in range(2)])
                            o = carve_at(QKV_OFF + 17920 + dr * 512, (128,), F32)
                            copy_any(o[:], pss[:, 0:128])
                            P.dma("sp", nst_d[s, l, dr, hd], o[:], "ostc%d" % dr, is_output=True)

        add_step(load1, comp1)
        add_step(load2, comp2)

    def outproj_steps(l, cnd):
        for s in range(4):
            def load(slot, s=s):
                sl = slab(slot, (NK, 512))
                src = wout_d[l].rearrange("(k p) c -> p k c", p=128)[:, :, s * 512:(s + 1) * 512]
                slab_dma(sl[:], src, "slab%d" % slot)

            def comp(slot, s=s):
                sl = slab(slot, (NK, 512))
                for j in range(4):
                    oc = 4 * s + j
                    for h in range(2):
                        hs = slice(h * 512, (h + 1) * 512)
                        ps = pst(pbank())

                        def post(ps=ps, oc=oc, hs=hs):
                            stt("dve", xT[:, oc, hs], ps[:], modA[:, l, cnd, 2, oc:oc + 1], xT[:, oc, hs], ALU.mult, ALU.add)
                        pipe_mm(ps[:], [(sl[:, k, j * 128:(j + 1) * 128], mixT[:, k, hs]) for k in range(NK)], post)
                if s == 3:
                    pipe_flush()
            add_step(load, comp)

    def ffn_steps(grp, l, cnd):
        nseq = 4 if grp == "P" else 1
        seqlen = T // nseq
        ngrp = (NH + 3) // 4
        aT = [carve_at(MIX_OFF + i * 8192, (4, T), BF16) for i in range(2)]
        uoffs = [MIX_OFF + 16384, MIX_OFF + 16384 + 4160, MIX_OFF + 16384 + 8320, MBG_OFF]
        upad = [[carve_at(uoffs[i * 2 + gv], (nseq, seqlen + 2), F32) for gv in range(2)] for i in range(2)]
        def zero_pads(slot):
            for i in range(2):
                for gv in range(2):
                    u = upad[i][gv]
                    P.op("dve", lambda e, u=u: e.memset(u[:, :, 0:1].ap, 0.0), writes=[u[:, :, 0:1]])
                    P.op("dve", lambda e, u=u: e.memset(u[:, :, seqlen + 1:seqlen + 2].ap, 0.0),
                         writes=[u[:, :, seqlen + 1:seqlen + 2]])
        add_step(None, zero_pads, ring=(0, 1, 2))
        cvt = [carve_at(S1_OFF + i * 4096, (nseq, seqlen), F32) for i in range(2)]
        ccount = [0]

        u_groups = []
        d_list = []
        for gi in range(ngrp):
            u_list = []
            c0 = gi * 4
            ncg = min(4, NH - c0)
            pairs = [(c0 + 2 * i, min(2, ncg - 2 * i)) for i in range((ncg + 1) // 2)]
            for pi, (cc0, ncp) in enumerate(pairs):
                def loadU(slot, cc0=cc0, ncp=ncp):
                    sl = slab(slot, (NK, 2, 256))
                    base = wup_d[l].rearrange("(k p) c -> p k c", p=128)
                    for gv in range(2):
                        col = gv * FFN + cc0 * 128
                        slab_dma(sl[:, :, gv, 0:ncp * 128], base[:, :, col:col + ncp * 128], "slab%d" % slot)

                def compU(slot, cc0=cc0, ncp=ncp, gi=gi, c0=c0):
                    sl = slab(slot, (NK, 2, 256))
                    for ci in range(ncp):
                        c = cc0 + ci
                        ib = ccount[0] % 2
                        ccount[0] += 1
                        for gv in range(2):
                            u = upad[ib][gv]
                            for h in range(2):
                                hs = slice(h * 512, (h + 1) * 512)
                                b = nrm_cnt[0] % 6
                                nrm_cnt[0] += 1
                                ps = pst(b)
                                mm(ps[:], [(sl[:, k, gv, ci * 128:(ci + 1) * 128], hT[:, k, hs]) for k in range(NK)])
                                if nseq == 4:
                                    act(u[:, 2 * h:2 * h + 2, 1:seqlen + 1], pst(b, (2, 256))[:], AF.Copy)
                                else:
                                    act(u[:, 0, 1 + h * 512:1 + (h + 1) * 512], ps[:], AF.Copy)
                            cb = convbT[:, l, gv * NH + c: gv * NH + c + 1]
                            t = cvt[gv]
                            act(t[:], u[:, :, 1:seqlen + 1], AF.Identity, scale=convwT[:, l, 1, gv * NH + c: gv * NH + c + 1], bias=cb)
                            stt("dve", t[:], u[:, :, 0:seqlen], convwT[:, l, 0, gv * NH + c: gv * NH + c + 1], t[:], ALU.mult, ALU.add)
                            stt("dve", t[:], u[:, :, 2:seqlen + 2], convwT[:, l, 2, gv * NH + c: gv * NH + c + 1], t[:], ALU.mult, ALU.add)
                        act(cvt[0][:], cvt[0][:], AF.Silu)
                        dst = aT[gi % 2][:, c - c0, :]
                        tt("pool", View(dst.ap.rearrange("p (s t) -> p s t", s=nseq), dst.name, dst.ranges), cvt[0][:], cvt[1][:], ALU.mult)
                u_list.append((loadU, compU))

            def loadD(slot, c0=c0, ncg=ncg):
                sl = slab(slot, (4, D))
                src = wdn_d[l][c0 * 128:(c0 + ncg) * 128, :].rearrange("(k p) c -> p k c", p=128)
                slab_dma(sl[:, 0:ncg, :], src, "slab%d" % slot)

            def compD(slot, c0=c0, ncg=ncg, gi=gi):
                sl = slab(slot, (4, D))
                a = aT[gi % 2]
                for oc in range(NK):
                    for h in range(2):
                        hs = slice(h * 512, (h + 1) * 512)
                        b = 6 + (nrm_cnt[0] % 2)
                        nrm_cnt[0] += 1
                        ps = pst(b)
                        mm(ps[:], [(sl[:, k, oc * 128:(oc + 1) * 128], a[:, k, hs]) for k in range(ncg)])
                        stt("dve", xT[:, oc, hs], ps[:], modA[:, l, cnd, 5, oc:oc + 1], xT[:, oc, hs], ALU.mult, ALU.add)
            d_list.append((loadD, compD))
            u_groups.append(u_list)

        for gi in range(ngrp):
            for (ld_, cp_) in u_groups[gi]:
                add_step(ld_, cp_, ring=(0, 1, 2))
            add_step(d_list[gi][0], d_list[gi][1], ring=(0, 1, 2))

    for grp in groups:
        cnd = 0 if grp == "P" else 1
        load_x_steps(xp_d if grp == "P" else xs_d)
        for l in layers:
            if do_mix:
                bg_require(l, 1)
                norm_mod_steps(l, cnd, 0, 1)
                bg_on[0] = True
                for hp in range(2):
                    unit_A_steps(grp, l, hp)
                for g in range(2):
                    unit_B_steps(grp, l, g)
                for hp in range(2):
                    unit_C_steps(grp, l, hp)
                bg_on[0] = False
                bg_require(l, 2)
                bg_on[0] = True
                outproj_steps(l, cnd)
                bg_on[0] = False
            if do_ffn:
                bg_require(l, 4)
                norm_mod_steps(l, cnd, 3, 4)
                bg_require(l, 5, ring=(0, 1, 2))
                bg_on[0] = True
                ffn_steps(grp, l, cnd)
                bg_on[0] = False
        store_y_steps(yp_d if grp == "P" else ys_d)

    LOOK = 2
    slot_last = {0: -3, 1: -2, 2: -1}
    step_slot = {}
    widx = [i for i, s in enumerate(steps) if s[0] is not None]
    nxt = [0]

    def ensure_loaded(upto_w, cur_i):
        while nxt[0] < len(widx) and nxt[0] <= upto_w:
            si = widx[nxt[0]]
            ld, _, ring = steps[si]
            allowed = [s_ for s_ in ring if slot_last[s_] < cur_i]
            if not allowed:
                break
            slot = min(allowed, key=lambda s_: slot_last[s_])
            slot_last[slot] = si
            step_slot[si] = slot
            ld(slot)
            nxt[0] += 1

    wpos = 0
    for i, (ld, comp, ring) in enumerate(steps):
        while wpos < len(widx) and widx[wpos] <= i:
            wpos += 1
        look = LOOK if len(ring) == 3 else 1
        ensure_loaded(wpos - 1 + look, i)
        assert ld is None or i in step_slot, i
        comp(step_slot.get(i))
        pipe_flush()

    P.finish()

    with nc.Block() as block:
        @block.tensor
        def _(e):
            P.replay("pe", e)

        @block.scalar
        def _(e):
            P.replay("act", e)

        @block.vector
        def _(e):
            P.replay("dve", e)

        @block.gpsimd
        def _(e):
            P.replay("pool", e)

        @block.sync
        def _(e):
            P.replay("sp", e)
    stack.close()
    return nc, P


def _consts():
    c = {}
    c["identf"] = np.eye(128, dtype=np.float32)
    perm = np.zeros((128, 128), np.float32)
    for m in range(128):
        blk = m // 32
        src = m + 32 if blk % 2 == 0 else m - 32
        perm[src, m] = 1.0
    c["perm"] = perm
    nf = 32
    inv_freq = np.power(np.float32(10000.0), -np.arange(nf, dtype=np.float32) / nf).astype(np.float32)
    t = np.arange(T)
    rows = (t // 64).astype(np.float32)
    cols = (t % 64).astype(np.float32)
    ang_r = rows[None, :] * inv_freq[:, None]
    ang_c = cols[None, :] * inv_freq[:, None]
    C = np.concatenate([np.cos(ang_r), np.cos(ang_r), np.cos(ang_c), np.cos(ang_c)], 0)
    S = np.concatenate([-np.sin(ang_r), np.sin(ang_r), -np.sin(ang_c), np.sin(ang_c)], 0)
    c["ropeC"] = C.astype(np.float32)
    c["ropeS"] = S.astype(np.float32)
    kc = np.arange(64)[:, None]
    qc = np.arange(64)[None, :]
    cs = np.clip(qc - 8, 0, 48)
    inw = (kc >= cs) & (kc < cs + 16)
    wm = np.where(inw, 0.0, -1e30).astype(np.float32)
    c["wmask"] = np.concatenate([wm, wm], 0)
    p = np.arange(128)[:, None]
    m = np.arange(1920)[None, :]
    X = (m - 896 - p).astype(np.float32)
    c["xpm"] = np.where(X >= 0, X, 1e9).astype(np.float32)
    c["xnm"] = np.where(X <= 0, -X, 1e9).astype(np.float32)
    tt_ = np.arange(T, dtype=np.float32)[None, :]
    c["t1f"] = np.broadcast_to(tt_ + 1.0, (128, T)).astype(np.float32).copy()
    c["t1b"] = np.broadcast_to(T - tt_, (128, T)).astype(np.float32).copy()
    pp = np.arange(128, dtype=np.float32)
    c["pos"] = np.stack([255 - pp, 127 - pp, pp, 128 + pp], 1).astype(np.float32)
    return c


def _pm(v, n):
    v = np.asarray(v, np.float32)
    lead = v.shape[:-1]
    r = v.reshape(lead + (n, 128))
    return np.ascontiguousarray(np.moveaxis(r, -1, 0))


_CACHE = {}


def kernel(x_prompt, x_sample, cache_ka, cache_va, cache_kb, cache_vb, state_ret, c, c_ctx,
           ada_w, ada_b, norm_mix_g, norm_ffn_g, w_in, q_norm_a, k_norm_a, q_norm_b, k_norm_b,
           na_rel_bias, ret_decay_logit, ret_norm_g, w_out, w_up, conv_w, conv_b, w_down, _cfg=None):
    f = lambda a: np.ascontiguousarray(np.asarray(a, dtype=np.float32))
    key = repr(_cfg)
    if key not in _CACHE:
        _CACHE[key] = build_program(_cfg)
    nc, _ = _CACHE[key]
    consts = _consts()
    shared = {
        "ada_w": f(ada_w), "w_in": f(w_in), "w_out": f(w_out), "w_up": f(w_up), "w_down": f(w_down),
        "adabT": _pm(ada_b, 96).reshape(128, -1),
        "gmixT": _pm(norm_mix_g, 16).reshape(128, -1),
        "gffnT": _pm(norm_ffn_g, 16).reshape(128, -1),
        "qkn": np.ascontiguousarray(np.stack([f(q_norm_a), f(k_norm_a), f(q_norm_b), f(k_norm_b)], -1).transpose(1, 0, 2)).reshape(128, -1),
        "retgT": _pm(ret_norm_g, 4).reshape(128, -1),
        "convwT": _pm(conv_w, 86).reshape(128, -1),
        "convbT": _pm(conv_b, 86).reshape(128, -1),
        "dlog": np.ascontiguousarray(np.broadcast_to(f(ret_decay_logit).reshape(1, 16), (128, 16))),
    }
    rb = f(na_rel_bias)
    dcidx = np.clip(np.arange(64)[:, None] - np.arange(64)[None, :] + 15, 0, 30)
    shared["mbf"] = np.ascontiguousarray(rb[:, :, :, dcidx])
    shared.update(consts)
    xpf, xsf = f(x_prompt), f(x_sample)
    cka, cva, ckb, cvb, sr = f(cache_ka), f(cache_va), f(cache_kb), f(cache_vb), f(state_ret)
    cf, ccf = f(c), f(c_ctx)
    in_maps = []
    for core in range(8):
        b = core // 2
        cond = np.stack([ccf, cf[b]], 0)
        m = dict(shared)
        m["xp"] = xpf[4 * core:4 * core + 4].reshape(T, D)
        m["xs"] = xsf[b]
        m["condT"] = np.ascontiguousarray(cond.reshape(2, 16, 128).transpose(2, 1, 0)).reshape(128, 32)
        m["cka"] = cka[b].reshape(2, 256, 512)
        m["cva"] = cva[b].reshape(2, 256, 512)
        m["ckb"] = ckb[b].reshape(2, 256, 256)
        m["cvb"] = cvb[b].reshape(2, 256, 256)
        m["sret"] = sr[b]
        in_maps.append(m)
    res = run_bass_kernel_spmd(nc, in_maps, core_ids=list(range(8)))
    R = res.results
    y_prompt = np.concatenate([R[i]["yp"].reshape(4, 256, D) for i in range(8)], 0)
    y_sample = np.stack([R[2 * b]["ys"] for b in range(4)], 0)
    nka = np.concatenate([R[i]["nka"].reshape(4, 2, 256, 4, 128) for i in range(8)], 0)
    nva = np.concatenate([R[i]["nva"].reshape(4, 2, 256, 4, 128) for i in range(8)], 0)
    nkb = np.concatenate([R[i]["nkb"].reshape(4, 2, 256, 2, 128) for i in range(8)], 0)
    nvb = np.concatenate([R[i]["nvb"].reshape(4, 2, 256, 2, 128) for i in range(8)], 0)
    nst = np.concatenate([R[i]["nst"] for i in range(8)], 0)
    return (y_prompt.astype(np.float32), y_sample.astype(np.float32), nka.astype(np.float32), nva.astype(np.float32),
            nkb.astype(np.float32), nvb.astype(np.float32), nst.astype(np.float32))
```
